# Optimizing a Trainium2 kernel written in Bass

```python
import jax, jax.numpy as jnp
from jax import lax
import numpy as np

D_MODEL = 1024
BATCH = 16
SEQ = 4096
DEPTH = 1

EPS = 1e-6
D_MIX = D_MODEL
D_RNN = D_MIX // 2
RNN_BLOCKS = 8
RNN_BW = D_RNN // RNN_BLOCKS
CONV_W = 4
LRU_C = 8.0
MLA_HEADS = 8
QK_NOPE = 64
QK_ROPE = 32
V_DIM = 64
Q_LORA = 256
KV_LORA = 128
D_ATT = MLA_HEADS * V_DIM
QK_DIM = QK_NOPE + QK_ROPE
ATT_SCALE = QK_DIM ** -0.5
ROPE_BASE = 10000.0
Q_BLOCK = 128
D_IN = 2 * D_RNN + Q_LORA + KV_LORA + QK_ROPE
IN_SPLITS = (D_RNN, 2 * D_RNN, 2 * D_RNN + Q_LORA, 2 * D_RNN + Q_LORA + KV_LORA)
D_FF = ((8 * D_MODEL // 3 + 255) // 256) * 256
PLE_DIM = 256
MAX_POS_OFFSET = 1024

kernel_name = 'hymba_hawk_mla_hybrid'


def rmsnorm(x, g):
    xf = x.astype(jnp.float32)
    y = xf * lax.rsqrt(jnp.mean(xf * xf, axis=-1, keepdims=True) + EPS)
    return (y * g.astype(jnp.float32)).astype(x.dtype)


def rope_cos_sin(positions, dim):
    inv = ROPE_BASE ** (-jnp.arange(0, dim, 2, dtype=jnp.float32) / dim)
    ang = positions.astype(jnp.float32)[..., None] * inv
    return jnp.cos(ang), jnp.sin(ang)


def apply_rope(t, cos, sin):
    t1, t2 = jnp.split(t.astype(jnp.float32), 2, axis=-1)
    out = jnp.concatenate([t1 * cos - t2 * sin, t1 * sin + t2 * cos], axis=-1)
    return out.astype(t.dtype)


def causal_depthwise_conv(x, w, b):
    c = x.shape[-1]
    y = lax.conv_general_dilated(
        x, w[:, None, :].astype(x.dtype), window_strides=(1,),
        padding=[(CONV_W - 1, 0)], dimension_numbers=('NWC', 'WIO', 'NWC'),
        feature_group_count=c)
    return y + b.astype(x.dtype)


def _lin_rec_combine(left, right):
    a1, b1 = left
    a2, b2 = right
    return a1 * a2, a2 * b1 + b2


def rglru_group(x_rnn, x_gate, conv_w, conv_b, w_a, b_a, w_x, b_x, lru_L):
    B, S, _ = x_rnn.shape
    xc = causal_depthwise_conv(x_rnn, conv_w, conv_b)
    xb = xc.reshape(B, S, RNN_BLOCKS, RNN_BW)
    r = jax.nn.sigmoid(jnp.einsum('bsnc,ncd->bsnd', xb, w_a) + b_a).reshape(B, S, D_RNN)
    i = jax.nn.sigmoid(jnp.einsum('bsnc,ncd->bsnd', xb, w_x) + b_x).reshape(B, S, D_RNN)
    log_a = -LRU_C * r.astype(jnp.float32) * jax.nn.softplus(-lru_L.astype(jnp.float32))
    a = jnp.exp(log_a)
    mult = jnp.sqrt(-jnp.expm1(2.0 * log_a))
    bterm = mult * (i * xc).astype(jnp.float32)
    _, h = lax.associative_scan(_lin_rec_combine, (a, bterm), axis=1)
    return h.astype(x_rnn.dtype) * jax.nn.gelu(x_gate)


def causal_mla_attention(q_nope, q_rope, k_nope, k_rope, v):
    B, S, H, _ = q_nope.shape
    nb = S // Q_BLOCK
    k_idx = jnp.arange(S)

    def to_blocks(t):
        return jnp.moveaxis(t.reshape(B, nb, Q_BLOCK, *t.shape[2:]), 1, 0)

    def one_block(args):
        qn, qr, blk = args
        s = jnp.einsum('bqhd,bkhd->bhqk', qn, k_nope, preferred_element_type=jnp.float32)
        s = s + jnp.einsum('bqhr,bkr->bhqk', qr, k_rope, preferred_element_type=jnp.float32)
        q_idx = blk * Q_BLOCK + jnp.arange(Q_BLOCK)
        mask = k_idx[None, :] <= q_idx[:, None]
        s = jnp.where(mask[None, None], s * ATT_SCALE, -jnp.inf)
        pr = jax.nn.softmax(s, axis=-1).astype(v.dtype)
        return jnp.einsum('bhqk,bkhd->bqhd', pr, v)

    out = lax.map(one_block, (to_blocks(q_nope), to_blocks(q_rope), jnp.arange(nb)))
    return jnp.moveaxis(out, 0, 1).reshape(B, S, H * V_DIM)


def setup_inputs(seed: int = 0) -> dict:
    key = jax.random.key(seed)
    ks = jax.random.split(key, 32)
    f32 = jnp.float32

    def nrm(k, shape, fan_in):
        return jax.random.normal(k, shape, f32) * fan_in ** -0.5

    def gain(k, shape):
        return 1.0 + 0.02 * jax.random.normal(k, shape, f32)

    def small(k, shape):
        return 0.01 * jax.random.normal(k, shape, f32)

    L = DEPTH
    x = jax.random.normal(ks[0], (BATCH, SEQ, D_MODEL), f32)
    p = jax.random.normal(ks[1], (DEPTH, BATCH, SEQ, PLE_DIM), f32)
    offs = jax.random.randint(ks[2], (BATCH, 1), 0, MAX_POS_OFFSET, dtype=jnp.int32)
    positions = (offs + jnp.arange(SEQ, dtype=jnp.int32)[None, :]).astype(jnp.int32)
    u = jax.random.uniform(ks[3], (L, D_RNN), f32, minval=0.9, maxval=0.999)
    a0 = u ** (1.0 / LRU_C)
    lru_L = jnp.log(a0) - jnp.log1p(-a0)
    return {
        'x': x,
        'p': p,
        'positions': positions,
        'g_mix': gain(ks[4], (L, D_MODEL)),
        'w_in': nrm(ks[5], (L, D_MODEL, D_IN), D_MODEL),
        'conv_w': nrm(ks[6], (L, CONV_W, D_RNN), CONV_W),
        'conv_b': small(ks[7], (L, D_RNN)),
        'w_rg_a': nrm(ks[8], (L, RNN_BLOCKS, RNN_BW, RNN_BW), RNN_BW),
        'b_rg_a': small(ks[9], (L, RNN_BLOCKS, RNN_BW)),
        'w_rg_x': nrm(ks[10], (L, RNN_BLOCKS, RNN_BW, RNN_BW), RNN_BW),
        'b_rg_x': small(ks[11], (L, RNN_BLOCKS, RNN_BW)),
        'lru_L': lru_L,
        'g_q_lat': gain(ks[12], (L, Q_LORA)),
        'w_q_up': nrm(ks[13], (L, Q_LORA, MLA_HEADS * QK_DIM), Q_LORA),
        'g_kv_lat': gain(ks[14], (L, KV_LORA)),
        'w_kv_up': nrm(ks[15], (L, KV_LORA, MLA_HEADS * (QK_NOPE + V_DIM)), KV_LORA),
        'g_out_rnn': gain(ks[16], (L, D_RNN)),
        'g_out_att': gain(ks[17], (L, D_ATT)),
        'w_out': nrm(ks[18], (L, D_MIX, D_MODEL), D_MIX),
        'g_ffn': gain(ks[19], (L, D_MODEL)),
        'w_ffn_gate': nrm(ks[20], (L, D_MODEL, D_FF), D_MODEL),
        'w_ffn_up': nrm(ks[21], (L, D_MODEL, D_FF), D_MODEL),
        'w_ffn_down': nrm(ks[22], (L, D_FF, D_MODEL), D_FF),
        'g_ple_in': gain(ks[23], (L, D_MODEL)),
        'w_ple_gate': nrm(ks[24], (L, D_MODEL, D_MODEL), D_MODEL),
        'w_ple_proj': nrm(ks[25], (L, PLE_DIM, D_MODEL), PLE_DIM),
        'g_ple_post': gain(ks[26], (L, D_MODEL)),
        'g_final': gain(ks[27], (D_MODEL,)),
    }


def reference(x, p, positions, g_mix, w_in, conv_w, conv_b, w_rg_a, b_rg_a, w_rg_x,
              b_rg_x, lru_L, g_q_lat, w_q_up, g_kv_lat, w_kv_up, g_out_rnn, g_out_att,
              w_out, g_ffn, w_ffn_gate, w_ffn_up, w_ffn_down, g_ple_in, w_ple_gate,
              w_ple_proj, g_ple_post, g_final):
    B, S, _ = x.shape
    cos, sin = rope_cos_sin(positions, QK_ROPE)
    h = x
    for l in range(DEPTH):
        u = rmsnorm(h, g_mix[l])
        z = u @ w_in[l]
        x_rnn, x_gate, c_q, c_kv, k_rope = jnp.split(z, IN_SPLITS, axis=-1)

        y_rnn = rglru_group(x_rnn, x_gate, conv_w[l], conv_b[l], w_rg_a[l], b_rg_a[l],
                            w_rg_x[l], b_rg_x[l], lru_L[l])

        q = (rmsnorm(c_q, g_q_lat[l]) @ w_q_up[l]).reshape(B, S, MLA_HEADS, QK_DIM)
        q_nope, q_rope = jnp.split(q, [QK_NOPE], axis=-1)
        q_rope = apply_rope(q_rope, cos[:, :, None, :], sin[:, :, None, :])
        kv = (rmsnorm(c_kv, g_kv_lat[l]) @ w_kv_up[l]).reshape(B, S, MLA_HEADS, QK_NOPE + V_DIM)
        k_nope, v = jnp.split(kv, [QK_NOPE], axis=-1)
        k_rope = apply_rope(k_rope, cos, sin)
        y_att = causal_mla_attention(q_nope, q_rope, k_nope, k_rope, v)

        y = jnp.concatenate([rmsnorm(y_rnn, g_out_rnn[l]), rmsnorm(y_att, g_out_att[l])], axis=-1)
        h = h + y @ w_out[l]

        vff = rmsnorm(h, g_ffn[l])
        h = h + (jax.nn.silu(vff @ w_ffn_gate[l]) * (vff @ w_ffn_up[l])) @ w_ffn_down[l]

        e = rmsnorm(p[l] @ w_ple_proj[l], g_ple_post[l])
        gate = jax.nn.sigmoid(rmsnorm(h, g_ple_in[l]) @ w_ple_gate[l])
        h = h + gate * e
    return rmsnorm(h, g_final)
```

```python
import math
from contextlib import ExitStack

import numpy as np
import concourse.bass as bass
import concourse.mybir as mybir
from concourse.bass_utils import run_bass_kernel_spmd

F32 = mybir.dt.float32
BF16 = mybir.dt.bfloat16
I32 = mybir.dt.int32
AF = mybir.ActivationFunctionType
ALU = mybir.AluOpType

NCORES = 8
SEQ = 4096
NTOK = 2 * SEQ
D = 1024
DFF = 2816
EPS = 1e-6
ATT_SCALE = 96 ** -0.5
MAGIC = 12582912.0
TWO_PI = 2.0 * math.pi
C1 = 6.28125
C2 = TWO_PI - C1

GC = {}
_c = 0
for _n, _w in (("g_mix", 8), ("g_ffn", 8), ("g_ple_in", 8), ("g_ple_post", 8), ("g_final", 8),
               ("conv_w", 16), ("conv_b", 4), ("b_a", 4), ("b_x", 4), ("lru", 4),
               ("g_q", 2), ("g_kv", 1), ("g_rnn", 4), ("g_att", 4), ("inv", 1), ("sgn", 1)):
    GC[_n] = _c
    _c += _w
NG = _c


class _Op:
    __slots__ = ("eng", "fn", "is_dma", "dkey", "waits", "signals", "sig_val", "pos")

    def __init__(self, eng, fn, is_dma, dkey):
        self.eng = eng
        self.fn = fn
        self.is_dma = is_dma
        self.dkey = dkey
        self.waits = []
        self.signals = False
        self.sig_val = None
        self.pos = 0


def deps_force(lst, eng):
    return [p for p in lst if p.is_dma or p.eng != eng]


class Prog:
    ENGS = ("pe", "dve", "act", "pool", "sp")

    def __init__(self, nc):
        self.nc = nc
        self.ops = {e: [] for e in self.ENGS}
        self.res = {}
        self.dma_counts = {}
        self.waited = {e: {} for e in self.ENGS}
        self.pending = {e: [] for e in self.ENGS}
        self.exempt = set()

    def _r(self, key):
        r = self.res.get(key)
        if r is None:
            r = self.res[key] = [None, {}]
        return r

    def barrier(self):
        lasts = []
        for e in self.ENGS:
            for o in reversed(self.ops[e]):
                if not o.is_dma:
                    lasts.append(o)
                    break
        dl = {}
        for e in self.ENGS:
            for o in self.ops[e]:
                if o.is_dma and o.dkey not in self.exempt:
                    dl[o.dkey] = o
        for e in self.ENGS:
            self.pending[e] = lasts + list(dl.values())

    def op(self, eng, fn, reads=(), writes=(), dma=None, marks=()):
        is_dma = dma is not None
        o = _Op(eng, fn, is_dma, dma)
        o.pos = len(self.ops[eng])
        deps = [(p, True) for p in self.pending[eng]]
        if deps:
            deps = [(p, True) for p in (self.pending[eng] if is_dma else deps_force(self.pending[eng], eng))]
            self.pending[eng] = []
        for k in reads:
            r = self._r(k)
            if r[0] is not None:
                deps.append((r[0], True))
        for k in writes:
            r = self._r(k)
            if r[0] is not None:
                deps.append((r[0], False))
            for rd in r[1].values():
                deps.append((rd, False))
        wd = self.waited[eng]
        for p, is_raw in deps:
            if p is o:
                continue
            if p.is_dma:
                key = ("d", p.dkey)
                if wd.get(key, 0) >= p.sig_val:
                    continue
                wd[key] = p.sig_val
                o.waits.append(p)
                continue
            if p.eng == eng and not is_dma:
                if eng == "pe":
                    continue
            key = ("e", p.eng)
            if wd.get(key, -1) >= p.pos:
                continue
            wd[key] = p.pos
            o.waits.append(p)
            p.signals = True
        for k in list(writes) + list(marks):
            r = self._r(k)
            r[0] = o
            r[1] = {}
        for k in reads:
            rk = ("dma", id(o)) if is_dma else eng
            self._r(k)[1][rk] = o
        if is_dma:
            c = self.dma_counts.get(dma, 0) + 1
            self.dma_counts[dma] = c
            o.sig_val = 16 * c
        self.ops[eng].append(o)
        return o

    def emit(self):
        nc = self.nc
        with ExitStack() as st:
            esem = {e: st.enter_context(nc.semaphore("s_" + e)) for e in self.ENGS}
            dsem = {k: st.enter_context(nc.semaphore("d_%s" % (k,))) for k in self.dma_counts}
            for e in self.ENGS:
                n = 0
                for o in self.ops[e]:
                    if not o.is_dma and o.signals:
                        n += 1
                        o.sig_val = n
            block = st.enter_context(nc.Block())

            def replay(engobj, e):
                for o in self.ops[e]:
                    for p in o.waits:
                        sem = dsem[p.dkey] if p.is_dma else esem[p.eng]
                        engobj.wait_ge(sem, p.sig_val)
                    ins = o.fn(engobj)
                    if o.is_dma:
                        ins.then_inc(dsem[o.dkey], 16)
                    elif o.signals:
                        ins.then_inc(esem[e], 1)
                if e == "sp":
                    for k, c in self.dma_counts.items():
                        engobj.wait_ge(dsem[k], 16 * c)

            @block.tensor
            def _(t):
                replay(t, "pe")

            @block.vector
            def _(v):
                replay(v, "dve")

            @block.scalar
            def _(s):
                replay(s, "act")

            @block.gpsimd
            def _(g):
                replay(g, "pool")

            @block.sync
            def _(s):
                replay(s, "sp")


def build_program(debug=False):
    nc = bass.Bass("TRN2", target_bir_lowering=False)
    SK = "ExternalOutput" if debug else "Internal"
    dt = nc.dram_tensor
    xT = dt("xT", [D, NTOK], F32, kind="ExternalInput").ap()
    pT = dt("pT", [256, NTOK], F32, kind="ExternalInput").ap()
    pos = dt("pos", [2, SEQ], I32, kind="ExternalInput").ap()
    gpk = dt("gpk", [128, NG], F32, kind="ExternalInput").ap()
    w_in = dt("w_in", [D, 1440], F32, kind="ExternalInput").ap()
    w_rg_a = dt("w_rg_a", [8, 64, 64], F32, kind="ExternalInput").ap()
    w_rg_x = dt("w_rg_x", [8, 64, 64], F32, kind="ExternalInput").ap()
    w_q_up = dt("w_q_up", [256, 768], F32, kind="ExternalInput").ap()
    w_kv_up = dt("w_kv_up", [128, 1024], F32, kind="ExternalInput").ap()
    w_out = dt("w_out", [D, D], F32, kind="ExternalInput").ap()
    w_g = dt("w_g", [D, DFF], F32, kind="ExternalInput").ap()
    w_u = dt("w_u", [D, DFF], F32, kind="ExternalInput").ap()
    w_d = dt("w_d", [DFF, D], F32, kind="ExternalInput").ap()
    w_pg = dt("w_pg", [D, D], F32, kind="ExternalInput").ap()
    w_pp = dt("w_pp", [256, D], F32, kind="ExternalInput").ap()
    outT = dt("outT", [D, NTOK], F32, kind="ExternalOutput").ap()
    yn_s = dt("yn_s", [512, NTOK], BF16, kind=SK).ap()
    q_s = dt("q_s", [96, 8, NTOK], BF16, kind=SK).ap()
    k_s = dt("k_s", [64, 8, NTOK], BF16, kind=SK).ap()
    kr_s = dt("kr_s", [32, NTOK], BF16, kind=SK).ap()
    v_s = dt("v_s", [128, NTOK // 128, 768], BF16, kind=SK).ap()
    h1_s = dt("h1_s", [D, NTOK], F32, kind=SK).ap()
    wg_b = dt("wg_b", [D, DFF], BF16, kind="Internal").ap()
    wu_b = dt("wu_b", [D, DFF], BF16, kind="Internal").ap()
    wd_b = dt("wd_b", [DFF, D], BF16, kind="Internal").ap()
    wpg_b = dt("wpg_b", [D, D], BF16, kind="Internal").ap()
    wpp_b = dt("wpp_b", [256, D], BF16, kind="Internal").ap()
    wout_b = dt("wout_b", [D, D], BF16, kind="Internal").ap()

    P = Prog(nc)
    op = P.op

    with ExitStack() as top:
        sb = lambda name, shape, dtp: top.enter_context(nc.sbuf_tensor(name, shape, dtp))
        ones = sb("ones", [128, 128], BF16)
        mask = sb("mask", [128, 128], BF16)
        gp = sb("gp", [128, NG], F32)
        cpar = sb("cpar", [128, 16], F32)
        psum = [top.enter_context(nc.psum_tensor("ps%d" % b, [128, 512], F32)) for b in range(8)]

        def G(name, j=0):
            c = GC[name] + j
            return gp[:, c:c + 1]

        op("sp", lambda e: e.dma_start(out=gp[:], in_=gpk[:, :]), writes=["gp"], dma="c0")
        op("dve", lambda e: e.memset(ones[:], 1.0), writes=["ones"])
        op("pool", lambda e: e.memset(mask[:], 1.0), writes=["mask"])
        op("pool", lambda e: e.affine_select(out=mask[:], in_=mask[:], pattern=[[1, 128]],
                                             compare_op=ALU.is_ge, fill=0.0, base=0,
                                             channel_multiplier=-1), reads=["mask"], writes=["mask"])
        L0 = GC["lru"]
        op("act", lambda e: e.activation(out=cpar[:, 0:4], in_=gp[:, L0:L0 + 4], func=AF.Exp, scale=-1.0),
           reads=["gp"], writes=["cpar"])
        op("act", lambda e: e.activation(out=cpar[:, 4:8], in_=cpar[:, 0:4], func=AF.Ln, bias=1.0),
           reads=["cpar"], writes=["cpar"])
        op("dve", lambda e: e.tensor_scalar(out=cpar[:, 0:4], in0=cpar[:, 4:8], scalar1=-8.0, scalar2=None,
                                            op0=ALU.mult), reads=["cpar"], writes=["cpar"])
        op("dve", lambda e: e.tensor_scalar(out=cpar[:, 4:8], in0=cpar[:, 0:4], scalar1=0.5, scalar2=None,
                                            op0=ALU.mult), reads=["cpar"], writes=["cpar"])
        Ba, Bx = GC["b_a"], GC["b_x"]
        op("dve", lambda e: e.tensor_scalar(out=cpar[:, 8:12], in0=gp[:, Ba:Ba + 4], scalar1=0.5, scalar2=None,
                                            op0=ALU.mult), reads=["gp", "cpar"], writes=["cpar"])
        op("dve", lambda e: e.tensor_scalar(out=cpar[:, 12:16], in0=gp[:, Bx:Bx + 4], scalar1=0.5, scalar2=None,
                                            op0=ALU.mult), reads=["gp", "cpar"], writes=["cpar"])

        bankrot = [0]

        def norm_rstd(chunks, dim, T, rstd, rstd_res, lnv, sq_res, bank):
            ps = psum[bank]
            n = len(chunks)
            for kc in range(n):
                op("pe", lambda e, kc=kc: e.matmul(ps[:, 0:T], lhsT=ones[:], rhs=chunks[kc],
                                                   start=(kc == 0), stop=(kc == n - 1)),
                   reads=["ones", sq_res], writes=[("ps", bank)])
            op("act", lambda e: e.activation(out=lnv[:, 0:T], in_=ps[:, 0:T], func=AF.Ln, scale=1.0 / dim, bias=EPS),
               reads=[("ps", bank)], writes=["lnv"])
            op("act", lambda e: e.activation(out=rstd[:, 0:T], in_=lnv[:, 0:T], func=AF.Exp, scale=-0.5),
               reads=["lnv"], writes=[rstd_res])

        T = 512
        NT1 = NTOK // T
        with ExitStack() as ph:
            sb1 = lambda name, shape, dtp: ph.enter_context(nc.sbuf_tensor(name, shape, dtp))
            Win = sb1("Win", [128, 8, 1440], BF16)
            Wkr = sb1("Wkr", [128, 8, 32], BF16)
            Wqn = sb1("Wqn", [128, 2, 8, 64], BF16)
            Wqr = sb1("Wqr", [128, 2, 2, 2, 4, 32], BF16)
            Wk = sb1("Wk", [128, 8, 64], BF16)
            Wv = sb1("Wv", [128, 8, 64], BF16)
            Wa = sb1("Wa", [128, 4, 128], BF16)
            Wx = sb1("Wx", [128, 4, 128], BF16)
            wstg = sb1("wstg", [128, 4, 128], F32)
            xs = sb1("xs", [128, 8, T], F32)
            sq = sb1("sq", [128, 8, T], BF16)
            xg = sb1("xg", [128, 8, T], BF16)
            rstd_m = sb1("rstd_m", [128, T], F32)
            rstd_x = sb1("rstd_x", [128, T], F32)
            lnv = sb1("lnv", [128, T], F32)
            xr = sb1("xr", [128, 4, T + 3], F32)
            hst = sb1("hst", [128, 4], F32)
            xc = sb1("xc", [128, 4, T], F32)
            xcb = sb1("xcb", [128, 4, T], BF16)
            thr = sb1("thr", [128, 4, T], F32)
            thi = sb1("thi", [128, 4, T], F32)
            av = sb1("av", [128, 4, T], F32)
            mu = sb1("mu", [128, 4, T], F32)
            hs = sb1("hs", [128, 4, T], F32)
            yr = sb1("yr", [128, 4, T], F32)
            yn = sb1("yn", [128, 4, T], BF16)
            cq = sb1("cq", [128, 2, T], F32)
            ckv = sb1("ckv", [128, T], F32)
            cqn = sb1("cqn", [128, 2, T], BF16)
            ckvn = sb1("ckvn", [128, T], BF16)
            Qst = sb1("Qst", [96, 8, T], BF16)
            Kst = sb1("Kst", [64, 8, T], BF16)
            Vst = sb1("Vst", [128, 4, 768], BF16)
            posi = sb1("posi", [128, T], I32)
            rp = sb1("rp", [128, 7, T], F32)
            rq = sb1("rq", [128, 2, T], F32)
            krf = sb1("krf", [32, T], BF16)
            sgq = sb1("sgq", [128, 1], F32)

            w_in_v = w_in.rearrange("(kc p) n -> p kc n", p=128)
            for kc in range(8):
                op("pool", lambda e, kc=kc: e.dma_start(out=Win[:, kc, :], in_=w_in_v[:, kc, :]),
                   writes=["Win"], dma="wWin")
                op("pool", lambda e, kc=kc: e.dma_start(out=Wkr[:, kc, 0:16], in_=w_in_v[:, kc, 1424:1440]),
                   writes=["Wkr"], dma="wWkr")
                op("pool", lambda e, kc=kc: e.dma_start(out=Wkr[:, kc, 16:32], in_=w_in_v[:, kc, 1408:1424]),
                   writes=["Wkr"], dma="wWkr")
            wq_v = w_q_up.rearrange("(kc p) (h d) -> p kc h d", p=128, d=96)
            for kc in range(2):
                op("pool", lambda e, kc=kc: e.dma_start(out=Wqn[:, kc, :, :], in_=wq_v[:, kc, :, 0:64]), writes=["Wqn"], dma="wWqn")
                for g in range(2):
                    hs_ = slice(4 * g, 4 * g + 4)
                    op("pool", lambda e, kc=kc, g=g, hs_=hs_: e.dma_start(out=Wqr[:, kc, g, 0, :, :], in_=wq_v[:, kc, hs_, 64:96]),
                       writes=["Wqr"], dma="wWqr")
                    op("pool", lambda e, kc=kc, g=g, hs_=hs_: e.dma_start(out=Wqr[:, kc, g, 1, :, 0:16], in_=wq_v[:, kc, hs_, 80:96]),
                       writes=["Wqr"], dma="wWqr")
                    op("pool", lambda e, kc=kc, g=g, hs_=hs_: e.dma_start(out=Wqr[:, kc, g, 1, :, 16:32], in_=wq_v[:, kc, hs_, 64:80]),
                       writes=["Wqr"], dma="wWqr")
            wkv_v = w_kv_up.rearrange("p (h e) -> p h e", e=128)
            op("pool", lambda e: e.dma_start(out=Wk[:, :, :], in_=wkv_v[:, :, 0:64]), writes=["Wk"], dma="wWk")
            op("pool", lambda e: e.dma_start(out=Wv[:, :, :], in_=wkv_v[:, :, 64:128]), writes=["Wv"], dma="wWv")
            for (Wt, wsrc, nm) in ((Wa, w_rg_a, "Wa"), (Wx, w_rg_x, "Wx")):
                op("dve", lambda e: e.memset(wstg[:], 0.0), writes=["wstg"])
                for ct in range(4):
                    op("sp", lambda e, ct=ct, wsrc=wsrc: e.dma_start(out=wstg[0:64, ct, 0:64], in_=wsrc[2 * ct, :, :]),
                       writes=["wstg"], dma="w2")
                    op("sp", lambda e, ct=ct, wsrc=wsrc: e.dma_start(out=wstg[64:128, ct, 64:128], in_=wsrc[2 * ct + 1, :, :]),
                       writes=["wstg"], dma="w2")
                op("dve", lambda e, Wt=Wt: e.tensor_copy(out=Wt[:], in_=wstg[:]), reads=["wstg"], writes=[nm])
            for tb in range(4):
                op("pool", lambda e, tb=tb: e.memset(Vst[:, tb, :].rearrange("p (m c) -> p m c", c=192)[:, :, 64:128], 1.0),
                   writes=["Vst"])
            cs_, ci_ = GC["sgn"], GC["inv"]
            op("dve", lambda e: e.tensor_scalar(out=sgq[:], in0=gp[:, cs_:cs_ + 1], scalar1=ATT_SCALE, scalar2=None, op0=ALU.mult),
               reads=["gp"], writes=["sgq"])

            def convert(src_ap, dst_ap, rows, cols, key):
                P.exempt.add(key)
                nsp = 1 if cols <= 2048 else 2
                cw = cols // nsp
                rstep = 256 if nsp == 2 else 512
                for r0 in range(0, rows, rstep):
                    r1 = min(rows, r0 + rstep)
                    sv = src_ap[r0:r1, :].rearrange("r (a b) -> r a b", b=cw)
                    dv = dst_ap[r0:r1, :].rearrange("r (a b) -> r a b", b=cw)
                    op("pool", lambda e, sv=sv, dv=dv: e.dma_start(out=dv, in_=sv), dma=key, marks=[key])
            def convert_all():
                convert(w_out, wout_b, D, D, "cv_wout")
                convert(w_g, wg_b, D, DFF, "cv_wg")
                convert(w_u, wu_b, D, DFF, "cv_wu")
                convert(w_d, wd_b, DFF, D, "cv_wd")
                convert(w_pp, wpp_b, 256, D, "cv_wpp")
                convert(w_pg, wpg_b, D, D, "cv_wpg")

            xT_v = xT.rearrange("(kc p) t -> p kc t", p=128)
            yn_v = yn_s.rearrange("(ct p) t -> p ct t", p=128)

            def load_x(i):
                t0 = i * T
                op("sp", lambda e: e.dma_start(out=xs[:], in_=xT_v[:, :, t0:t0 + T]), writes=["xs"], dma="xs")
                s = i // 8
                p0 = (i % 8) * T
                op("sp", lambda e: e.dma_start(out=posi[:], in_=pos[s:s + 1, p0:p0 + T].partition_broadcast(128)),
                   writes=["posi"], dma="pos")

            def proj(out_ps_ap, wsel, bank, nk=8, rd=("Win", "xg"), src=None):
                src = xg if src is None else src
                for kc in range(nk):
                    op("pe", lambda e, kc=kc: e.matmul(out_ps_ap, lhsT=wsel(kc), rhs=src[:, kc, :],
                                                       start=(kc == 0), stop=(kc == nk - 1)),
                       reads=[rd[0], (rd[1], kc)], writes=[("ps", bank)])

            def nb():
                b = bankrot[0]
                bankrot[0] = (b + 1) % 7
                return b

            TH, KK, RR, SIN, COS, SINQ, COSQ = [rp[:, j, :] for j in range(7)]
            CW, CB = GC["conv_w"], GC["conv_b"]
            def s0(i):
                first = (i % 8 == 0)
                op("act", lambda e: e.activation(out=sq[:], in_=xs[:], func=AF.Square), reads=["xs"], writes=["sq"])
                for kc in range(8):
                    op("act", lambda e, kc=kc: e.activation(out=xg[:, kc, :], in_=xs[:, kc, :], func=AF.Copy, scale=G("g_mix", kc)),
                       reads=["xs", "gp"], writes=[("xg", kc)])
                if first:
                    op("pool", lambda e: e.memset(xr[:, :, 0:3], 0.0), writes=["xr0", "xr1", "xr2", "xr3"])
                    op("pool", lambda e: e.memset(hst[:], 0.0), writes=[("hst", c_) for c_ in range(4)])

            def s0b():
                norm_rstd([sq[:, kc, :] for kc in range(8)], 1024.0, T, rstd_m, "rstd_m", lnv, "sq", 7)

            load_x(0)
            R4 = range(4)
            s0(0)
            s0b()
            for i in range(NT1):
                t0 = i * T
                if i == 1:
                    convert_all()
                op("act", lambda e: e.activation(out=TH, in_=posi[:], func=AF.Copy, scale=gp[:, ci_:ci_ + 1]),
                   reads=["posi", "gp"], writes=["rp_th"])
                xb = []
                for ct in R4:
                    b = nb()
                    xb.append(b)
                    proj(psum[b][:, :], lambda kc, ct=ct: Win[:, kc, ct * 128:(ct + 1) * 128], b)
                op("dve", lambda e: e.tensor_scalar(out=KK, in0=TH, scalar1=1.0 / TWO_PI, scalar2=MAGIC, op0=ALU.mult, op1=ALU.add),
                   reads=["rp_th"], writes=["rp_k"])
                op("dve", lambda e: e.tensor_scalar(out=KK, in0=KK, scalar1=MAGIC, scalar2=None, op0=ALU.subtract),
                   reads=["rp_k"], writes=["rp_k"])
                op("dve", lambda e: e.scalar_tensor_tensor(out=RR, in0=KK, scalar=-C1, in1=TH, op0=ALU.mult, op1=ALU.add),
                   reads=["rp_k", "rp_th"], writes=["rp_r"])
                op("dve", lambda e: e.scalar_tensor_tensor(out=RR, in0=KK, scalar=-C2, in1=RR, op0=ALU.mult, op1=ALU.add),
                   reads=["rp_k", "rp_r"], writes=["rp_r"])
                op("dve", lambda e: e.tensor_scalar(out=RR, in0=RR, scalar1=3.1415925, scalar2=-3.1415925, op0=ALU.min, op1=ALU.max),
                   reads=["rp_r"], writes=["rp_r"])
                op("act", lambda e: e.activation(out=COS, in_=RR, func=AF.Sin), reads=["rp_r"], writes=["rp_cos"])
                for ct in R4:
                    b = xb[ct]
                    op("dve", lambda e, ct=ct, b=b: e.tensor_tensor(out=xr[:, ct, 3:3 + T], in0=psum[b][:, :], in1=rstd_m[:], op=ALU.mult),
                       reads=[("ps", b), "rstd_m"], writes=["xr%d" % ct])
                cqb = []
                for kc2 in range(2):
                    b = nb()
                    cqb.append(b)
                    proj(psum[b][:, :], lambda kc, kc2=kc2: Win[:, kc, 1024 + kc2 * 128:1024 + (kc2 + 1) * 128], b)
                bkv = nb()
                proj(psum[bkv][:, :], lambda kc: Win[:, kc, 1280:1408], bkv)
                for ct in R4:
                    op("dve", lambda e, ct=ct: e.tensor_scalar(out=xc[:, ct, :], in0=xr[:, ct, 3:3 + T],
                                                              scalar1=gp[:, CW + 12 + ct:CW + 13 + ct],
                                                              scalar2=gp[:, CB + ct:CB + ct + 1], op0=ALU.mult, op1=ALU.add),
                       reads=["xr%d" % ct, "gp"], writes=[("xc", ct)])
                for k in (2, 1, 0):
                    for ct in R4:
                        op("dve", lambda e, ct=ct, k=k: e.scalar_tensor_tensor(
                            out=xc[:, ct, :], in0=xr[:, ct, k:k + T], scalar=gp[:, CW + 4 * k + ct:CW + 4 * k + ct + 1], in1=xc[:, ct, :],
                            op0=ALU.mult, op1=ALU.add), reads=["xr%d" % ct, "gp", ("xc", ct)], writes=[("xc", ct)])
                for ct in R4:
                    op("pool", lambda e, ct=ct: e.tensor_copy(out=xr[:, ct, 0:3], in_=xr[:, ct, T:T + 3]), reads=["xr%d" % ct], writes=["xr%d" % ct])
                op("act", lambda e: e.activation(out=SIN, in_=COS, func=AF.Copy, scale=gp[:, cs_:cs_ + 1]),
                   reads=["rp_cos", "gp"], writes=["rp_sin"])
                op("act", lambda e: e.activation(out=SINQ, in_=COS, func=AF.Copy, scale=sgq[:]),
                   reads=["rp_cos", "sgq"], writes=["rp_sinq"])
                for ct in R4:
                    op("act", lambda e, ct=ct: e.activation(out=xcb[:, ct, :], in_=xc[:, ct, :], func=AF.Copy), reads=[("xc", ct)], writes=[("xcb", ct)])
                for ct in R4:
                    b1 = nb()
                    op("pe", lambda e, ct=ct, b1=b1: e.matmul(psum[b1][:, :], lhsT=Wa[:, ct, :], rhs=xcb[:, ct, :], start=True, stop=True),
                       reads=["Wa", ("xcb", ct)], writes=[("ps", b1)])
                    op("act", lambda e, ct=ct, b1=b1: e.activation(out=thr[:, ct, :], in_=psum[b1][:, :], func=AF.Tanh,
                                                                   scale=0.5, bias=cpar[:, 8 + ct:9 + ct]),
                       reads=[("ps", b1), "cpar"], writes=[("thr", ct)])
                for kc2 in range(2):
                    b = cqb[kc2]
                    op("dve", lambda e, kc2=kc2, b=b: e.tensor_tensor(out=cq[:, kc2, :], in0=psum[b][:, :], in1=rstd_m[:], op=ALU.mult),
                       reads=[("ps", b), "rstd_m"], writes=["cq"])
                op("dve", lambda e, b=bkv: e.tensor_tensor(out=ckv[:], in0=psum[b][:, :], in1=rstd_m[:], op=ALU.mult),
                   reads=[("ps", bkv), "rstd_m"], writes=["ckv"])
                for ct in R4:
                    b2 = nb()
                    op("pe", lambda e, ct=ct, b2=b2: e.matmul(psum[b2][:, :], lhsT=Wx[:, ct, :], rhs=xcb[:, ct, :], start=True, stop=True),
                       reads=["Wx", ("xcb", ct)], writes=[("ps", b2)])
                    op("act", lambda e, ct=ct, b2=b2: e.activation(out=thi[:, ct, :], in_=psum[b2][:, :], func=AF.Tanh,
                                                                   scale=0.5, bias=cpar[:, 12 + ct:13 + ct]),
                       reads=[("ps", b2), "cpar"], writes=[("thi", ct)])
                op("dve", lambda e: e.tensor_scalar(out=KK, in0=RR, scalar1=math.pi / 2, scalar2=None, op0=ALU.add),
                   reads=["rp_r", "rp_k"], writes=["rp_k"])
                op("dve", lambda e: e.tensor_scalar(out=RR, in0=KK, scalar1=math.pi, scalar2=-TWO_PI, op0=ALU.is_gt, op1=ALU.mult),
                   reads=["rp_k", "rp_r"], writes=["rp_r"])
                op("dve", lambda e: e.tensor_tensor(out=KK, in0=KK, in1=RR, op=ALU.add), reads=["rp_k", "rp_r"], writes=["rp_k"])
                op("dve", lambda e: e.tensor_scalar(out=KK, in0=KK, scalar1=3.1415925, scalar2=-3.1415925, op0=ALU.min, op1=ALU.max),
                   reads=["rp_k"], writes=["rp_k"])
                op("act", lambda e: e.activation(out=sq[:, 0:2, :], in_=cq[:], func=AF.Square), reads=["cq"], writes=["sq"])
                op("act", lambda e: e.activation(out=sq[:, 2, :], in_=ckv[:], func=AF.Square), reads=["ckv"], writes=["sq"])
                op("act", lambda e: e.activation(out=COS, in_=KK, func=AF.Sin), reads=["rp_k", "rp_sin", "rp_sinq"], writes=["rp_cos"])
                op("act", lambda e: e.activation(out=COSQ, in_=COS, func=AF.Copy, scale=ATT_SCALE), reads=["rp_cos"], writes=["rp_cosq"])
                bA = nb()
                proj(psum[bA][0:32, :], lambda kc: Win[:, kc, 1408:1440], bA)
                bB = nb()
                proj(psum[bB][0:32, :], lambda kc: Wkr[:, kc, :], bB, rd=("Wkr", "xg"))
                gbk = []
                for ct in R4:
                    b = nb()
                    gbk.append(b)
                    proj(psum[b][:, :], lambda kc, ct=ct: Win[:, kc, 512 + ct * 128:512 + (ct + 1) * 128], b)
                norm_rstd([sq[:, kc, :] for kc in range(2)], 256.0, T, rstd_x, "rstd_x", lnv, "sq", 7)
                for kc in range(2):
                    op("dve", lambda e, kc=kc: e.scalar_tensor_tensor(out=cqn[:, kc, :], in0=cq[:, kc, :], scalar=G("g_q", kc),
                                                                     in1=rstd_x[:], op0=ALU.mult, op1=ALU.mult),
                       reads=["cq", "gp", "rstd_x"], writes=[("cqn", kc)])
                norm_rstd([sq[:, 2, :]], 128.0, T, rstd_x, "rstd_x", lnv, "sq", 7)
                op("dve", lambda e: e.scalar_tensor_tensor(out=ckvn[:], in0=ckv[:], scalar=G("g_kv"), in1=rstd_x[:],
                                                           op0=ALU.mult, op1=ALU.mult),
                   reads=["ckv", "gp", "rstd_x"], writes=["ckvn"])
                for ct in R4:
                    op("act", lambda e, ct=ct: e.activation(out=av[:, ct, :], in_=thr[:, ct, :], func=AF.Exp,
                                                            scale=cpar[:, 4 + ct:5 + ct], bias=cpar[:, 4 + ct:5 + ct]),
                       reads=[("thr", ct), "cpar"], writes=[("av", ct)])
                for ct in R4:
                    op("act", lambda e, ct=ct: e.activation(out=mu[:, ct, :], in_=thr[:, ct, :], func=AF.Exp,
                                                            scale=cpar[:, ct:ct + 1], bias=cpar[:, ct:ct + 1]),
                       reads=[("thr", ct), "cpar"], writes=[("mu", ct)])
                if i + 1 < NT1:
                    load_x(i + 1)
                op("dve", lambda e, b=bA: e.tensor_tensor(out=rq[0:32, 0, :], in0=psum[b][0:32, :], in1=COS[0:32, :], op=ALU.mult),
                   reads=[("ps", bA), "rp_cos"], writes=["rq"])
                op("dve", lambda e, b=bB: e.tensor_tensor(out=rq[0:32, 1, :], in0=psum[b][0:32, :], in1=SIN[0:32, :], op=ALU.mult),
                   reads=[("ps", bB), "rp_sin", "rq"], writes=["rq"])
                op("dve", lambda e: e.tensor_tensor(out=rq[0:32, 0, :], in0=rq[0:32, 0, :], in1=rq[0:32, 1, :], op=ALU.add),
                   reads=["rq"], writes=["rq"])
                op("dve", lambda e: e.tensor_tensor(out=krf[:], in0=rq[0:32, 0, :], in1=rstd_m[0:32, :], op=ALU.mult),
                   reads=["rq", "rstd_m"], writes=["krf"])
                op("sp", lambda e, t0=t0: e.dma_start(out=kr_s[:, t0:t0 + T], in_=krf[:]), reads=["krf"], dma="stkr", marks=["kr_all"])
                for ct in R4:
                    b = gbk[ct]
                    op("dve", lambda e, ct=ct, b=b: e.tensor_tensor(out=yr[:, ct, :], in0=psum[b][:, :], in1=rstd_m[:], op=ALU.mult),
                       reads=[("ps", b), "rstd_m"], writes=["yr%d" % ct])
                for ct in R4:
                    op("dve", lambda e, ct=ct: e.scalar_tensor_tensor(out=thi[:, ct, :], in0=thi[:, ct, :], scalar=1.0, in1=xc[:, ct, :],
                                                                     op0=ALU.add, op1=ALU.mult),
                       reads=[("thi", ct), ("xc", ct)], writes=[("thi", ct)])
                qb = []
                for h in range(8):
                    b = nb()
                    proj(psum[b][0:64, :], lambda kc, h=h: Wqn[:, kc, h, :], b, nk=2, rd=("Wqn", "cqn"), src=cqn)
                    op("act", lambda e, h=h, b=b: e.activation(out=Qst[0:64, h, :], in_=psum[b][0:64, :], func=AF.Copy, scale=ATT_SCALE),
                       reads=[("ps", b)], writes=[("Qst", h, 0)])
                    if h == 3:
                        for ct in R4:
                            op("act", lambda e, ct=ct: e.activation(out=mu[:, ct, :], in_=mu[:, ct, :], func=AF.Ln, scale=-1.0, bias=1.0),
                               reads=[("mu", ct)], writes=[("mu", ct)])
                for ct in R4:
                    op("act", lambda e, ct=ct: e.activation(out=mu[:, ct, :], in_=mu[:, ct, :], func=AF.Exp, scale=0.5),
                       reads=[("mu", ct)], writes=[("mu", ct)])
                for g in range(2):
                    bA2 = nb()
                    proj(psum[bA2][:, :], lambda kc, g=g: Wqr[:, kc, g, 0, :, :].rearrange("p h d -> p (h d)"), bA2, nk=2, rd=("Wqr", "cqn"), src=cqn)
                    op("dve", lambda e, bA2=bA2: e.tensor_tensor(out=rq[:, 0, :], in0=psum[bA2][:, :], in1=COSQ, op=ALU.mult),
                       reads=[("ps", bA2), "rp_cosq", "rq"], writes=["rq"])
                    bB2 = nb()
                    proj(psum[bB2][:, :], lambda kc, g=g: Wqr[:, kc, g, 1, :, :].rearrange("p h d -> p (h d)"), bB2, nk=2, rd=("Wqr", "cqn"), src=cqn)
                    op("dve", lambda e, bB2=bB2: e.tensor_tensor(out=rq[:, 1, :], in0=psum[bB2][:, :], in1=SINQ, op=ALU.mult),
                       reads=[("ps", bB2), "rp_sinq", "rq"], writes=["rq"])
                    for hh in range(4):
                        h = 4 * g + hh
                        op("dve", lambda e, h=h, hh=hh: e.tensor_tensor(out=Qst[64:96, h, :], in0=rq[32 * hh:32 * hh + 32, 0, :],
                                                                         in1=rq[32 * hh:32 * hh + 32, 1, :], op=ALU.add),
                           reads=["rq"], writes=[("Qst", h, 1)])
                op("sp", lambda e, t0=t0: e.dma_start(out=q_s[:, :, t0:t0 + T], in_=Qst[:]), reads=[("Qst", h_, z_) for h_ in range(8) for z_ in range(2)], dma="stq", marks=["q_all"])
                for ct in R4:
                    op("dve", lambda e, ct=ct: e.scalar_tensor_tensor(out=thi[:, ct, :], in0=thi[:, ct, :], scalar=0.5, in1=mu[:, ct, :],
                                                                     op0=ALU.mult, op1=ALU.mult),
                       reads=[("thi", ct), ("mu", ct)], writes=[("thi", ct)])
                for ct in R4:
                    op("dve", lambda e, ct=ct: e.tensor_tensor_scan(out=hs[:, ct, :], data0=av[:, ct, :], data1=thi[:, ct, :],
                                                                   initial=hst[:, ct:ct + 1], op0=ALU.mult, op1=ALU.add),
                       reads=[("av", ct), ("thi", ct), ("hst", ct)], writes=[("hs", ct)])
                    op("pool", lambda e, ct=ct: e.tensor_copy(out=hst[:, ct:ct + 1], in_=hs[:, ct, T - 1:T]),
                       reads=[("hs", ct)], writes=[("hst", ct)])
                if i + 1 < NT1:
                    s0(i + 1)
                for h in range(8):
                    b = nb()
                    op("pe", lambda e, h=h, b=b: e.matmul(psum[b][0:64, :], lhsT=Wk[:, h, :], rhs=ckvn[:], start=True, stop=True),
                       reads=["Wk", "ckvn"], writes=[("ps", b)])
                    op("act", lambda e, h=h, b=b: e.activation(out=Kst[:, h, :], in_=psum[b][0:64, :], func=AF.Copy),
                       reads=[("ps", b)], writes=[("Kst", h)])
                op("sp", lambda e, t0=t0: e.dma_start(out=k_s[:, :, t0:t0 + T], in_=Kst[:]), reads=[("Kst", h_) for h_ in range(8)], dma="stk", marks=["k_all"])
                for ct in R4:
                    op("act", lambda e, ct=ct: e.activation(out=yr[:, ct, :], in_=yr[:, ct, :], func=AF.Gelu_apprx_tanh),
                       reads=["yr%d" % ct], writes=["yr%d" % ct])
                for tb in range(4):
                    b = nb()
                    op("pe", lambda e, tb=tb, b=b: e.matmul(psum[b][:, :], lhsT=ckvn[:, tb * 128:(tb + 1) * 128],
                                                            rhs=Wv[:].rearrange("p h d -> p (h d)"), start=True, stop=True),
                       reads=["Wv", "ckvn"], writes=[("ps", b)])
                    for par in range(2):
                        c0 = 0 if par == 0 else 128
                        op("dve", lambda e, tb=tb, b=b, par=par, c0=c0: e.tensor_copy(
                            out=Vst[:, tb, :].rearrange("p (m c) -> p m c", c=192)[:, :, c0:c0 + 64],
                            in_=psum[b][:, :].rearrange("p (m t d) -> p m t d", t=2, d=64)[:, :, par, :]),
                           reads=[("ps", b)], writes=[("Vst", tb, par)])
                op("sp", lambda e, i=i: e.dma_start(out=v_s[:, i * 4:(i + 1) * 4, :], in_=Vst[:]), reads=["Vst"] + [("Vst", a_, b_) for a_ in range(4) for b_ in range(2)], dma="stv", marks=["v_all"])
                if i + 1 < NT1:
                    s0b()
                for ct in R4:
                    op("dve", lambda e, ct=ct: e.tensor_tensor(out=yr[:, ct, :], in0=hs[:, ct, :], in1=yr[:, ct, :], op=ALU.mult),
                       reads=[("hs", ct), "yr%d" % ct], writes=["yr%d" % ct])
                op("act", lambda e: e.activation(out=sq[:, 4:8, :], in_=yr[:], func=AF.Square), reads=["yr0", "yr1", "yr2", "yr3"], writes=["sq"])
                norm_rstd([sq[:, 4 + c, :] for c in range(4)], 512.0, T, rstd_x, "rstd_x", lnv, "sq", 7)
                for ct in R4:
                    op("dve", lambda e, ct=ct: e.scalar_tensor_tensor(out=yn[:, ct, :], in0=yr[:, ct, :], scalar=G("g_rnn", ct),
                                                                     in1=rstd_x[:], op0=ALU.mult, op1=ALU.mult),
                       reads=["yr%d" % ct, "gp", "rstd_x"], writes=[("yn", ct)])
                op("sp", lambda e, t0=t0: e.dma_start(out=yn_v[:, :, t0:t0 + T], in_=yn[:]), reads=[("yn", c_) for c_ in range(4)], dma="sty", marks=["y_all"])
        P.barrier()

        with ExitStack() as ph:
            sb1 = lambda name, shape, dtp: ph.enter_context(nc.sbuf_tensor(name, shape, dtp))
            KT = sb1("KT", [96, 8, SEQ], BF16)
            V = sb1("V", [128, 32, 768], BF16)
            Wout = sb1("Wout", [128, 8, D], BF16)
            Qt = sb1("Qt", [96, 2, 8, T], BF16)
            NPT = 6
            PT = sb1("PT", [128, NPT, T], BF16)
            yatt = sb1("yatt", [128, 4, T], F32)
            rc = sb1("rc", [128, 2, T], F32)
            yan = sb1("yan", [128, 4, T], BF16)
            ynr = sb1("ynr", [128, 4, T], BF16)
            NXO = 4
            xo = sb1("xo", [128, NXO, T], F32)
            sqa = sb1("sqa", [128, 4, T], BF16)
            rstd_a = sb1("rstd_a", [128, T], F32)
            lnv2 = sb1("lnv2", [128, T], F32)

            w_out_v = wout_b.rearrange("(kc p) n -> p kc n", p=128)
            op("sp", lambda e: e.dma_start(out=Wout[:], in_=w_out_v[:, :, :]), reads=["cv_wout"], writes=["Wout"], dma="wWout")
            xo_rot = [0]
            pt_rot = [0]
            s_rot = [0]
            LA = 3

            def tail_a():
                op("act", lambda e: e.activation(out=sqa[:], in_=yatt[:], func=AF.Square), reads=[("yatt", m_, z_) for m_ in range(4) for z_ in range(2)], writes=["sqa"])
                norm_rstd([sqa[:, c, :] for c in range(4)], 512.0, T, rstd_a, "rstd_a", lnv2, "sqa", 6)
                for m in range(4):
                    op("dve", lambda e, m=m: e.scalar_tensor_tensor(out=yan[:, m, :], in0=yatt[:, m, :], scalar=G("g_att", m),
                                                                   in1=rstd_a[:], op0=ALU.mult, op1=ALU.mult),
                       reads=[("yatt", m, 0), ("yatt", m, 1), "gp", "rstd_a"], writes=[("yan", m)])

            def tail_b(t0, oc):
                xsl = xo_rot[0]
                xo_rot[0] = (xsl + 1) % NXO
                pb = 6 + (oc % 2)
                op("sp", lambda e: e.dma_start(out=xo[:, xsl, :], in_=xT[oc * 128:(oc + 1) * 128, t0:t0 + T]),
                   writes=[("xo", xsl)], dma=("xo", xsl))
                for kc in range(8):
                    rhs = ynr[:, kc, :] if kc < 4 else yan[:, kc - 4, :]
                    op("pe", lambda e, kc=kc, rhs=rhs: e.matmul(psum[pb][:, :], lhsT=Wout[:, kc, oc * 128:(oc + 1) * 128],
                                                               rhs=rhs, start=(kc == 0), stop=(kc == 7)),
                       reads=["Wout", ("ynr" if kc < 4 else ("yan", kc - 4))], writes=[("ps", pb)])
                op("dve", lambda e: e.tensor_tensor(out=xo[:, xsl, :], in0=psum[pb][:, :], in1=xo[:, xsl, :], op=ALU.add),
                   reads=[("ps", pb), ("xo", xsl)], writes=[("xo", xsl)])
                op("sp", lambda e: e.dma_start(out=h1_s[oc * 128:(oc + 1) * 128, t0:t0 + T], in_=xo[:, xsl, :]),
                   reads=[("xo", xsl)], dma=("xst", xsl), marks=[("h1_all", xsl)])

            def load_q(gi):
                t0 = gi * T
                qsl = gi % 2
                op("sp", lambda e: e.dma_start(out=Qt[:, qsl, :, :], in_=q_s[:, :, t0:t0 + T]), reads=["q_all"], writes=[("Qt", qsl)], dma=("qt", qsl))

            def load_ynr(gi):
                t0 = gi * T
                op("sp", lambda e: e.dma_start(out=ynr[:], in_=yn_v[:, :, t0:t0 + T]), reads=["y_all"], writes=["ynr"], dma="ynr")

            pending = None
            load_q(0)
            for s in range(2):
                base = s * SEQ
                for c in range(8):
                    op("sp", lambda e, c=c, base=base: e.dma_start(out=KT[0:64, :, c * T:(c + 1) * T],
                                                                   in_=k_s[:, :, base + c * T:base + (c + 1) * T]),
                       reads=["k_all"], writes=[("KT", c)], dma=("kt", c))
                    for h in range(8):
                        op("sp", lambda e, c=c, base=base, h=h: e.dma_start(out=KT[64:96, h, c * T:(c + 1) * T],
                                                                            in_=kr_s[:, base + c * T:base + (c + 1) * T]),
                           reads=["kr_all"], writes=([("KTr", c)] if h == 0 else []), marks=([("KTr", c)] if h > 0 else []),
                           dma=("ktr", c))
                    kb0 = base // 128 + c * 4
                    op("sp", lambda e, c=c, kb0=kb0: e.dma_start(out=V[:, c * 4:(c + 1) * 4, :], in_=v_s[:, kb0:kb0 + 4, :]),
                       reads=["v_all"], writes=[("V", c)], dma=("vt", c))
                for i in range(8):
                    gi = s * 8 + i
                    t0 = base + i * T
                    qsl = gi % 2
                    if gi + 1 < 16:
                        load_q(gi + 1)
                    if pending is not None:
                        load_ynr(pending // T)
                    nkb = 4 * (i + 1)
                    blocks = [(h, kb) for h in range(8) for kb in range(nkb)]
                    nblk = len(blocks)
                    info = [None] * nblk
                    sched = {}
                    if pending is not None:
                        sched[3] = [("a", None)]
                        for oc in range(8):
                            sched.setdefault(6 + 2 * oc, []).append(("b", oc))
                    for idx in range(nblk + LA):
                        if idx < nblk:
                            h, kb = blocks[idx]
                            r = kb - 4 * i
                            c0 = 128 * r if r > 0 else 0
                            sbk = s_rot[0]
                            s_rot[0] = (sbk + 1) % 4
                            pts = pt_rot[0]
                            pt_rot[0] = (pts + 1) % NPT
                            info[idx] = (sbk, pts, c0)
                            kc_ = kb // 4
                            op("pe", lambda e, h=h, kb=kb, c0=c0, sbk=sbk, qsl=qsl: e.matmul(
                                psum[sbk][:, c0:T], lhsT=KT[:, h, kb * 128:(kb + 1) * 128], rhs=Qt[:, qsl, h, c0:T], start=True, stop=True),
                               reads=[("KT", kc_), ("KTr", kc_), ("Qt", qsl)], writes=[("ps", sbk)])
                            op("act", lambda e, c0=c0, sbk=sbk, pts=pts: e.activation(out=PT[:, pts, c0:T], in_=psum[sbk][:, c0:T], func=AF.Exp),
                               reads=[("ps", sbk)], writes=[("PT", pts)])
                            if r >= 0:
                                op("pool", lambda e, c0=c0, pts=pts: e.tensor_tensor(out=PT[:, pts, c0:c0 + 128], in0=PT[:, pts, c0:c0 + 128],
                                                                                     in1=mask[:], op=ALU.mult),
                                   reads=[("PT", pts), "mask"], writes=[("PT", pts)])
                        j = idx - LA
                        if j >= 0:
                            h, kb = blocks[j]
                            sbk, pts, c0 = info[j]
                            ob = 4 + (h % 2)
                            m = h // 2
                            voff = m * 192 + (0 if h % 2 == 0 else 64)
                            kc_ = kb // 4
                            op("pe", lambda e, kb=kb, c0=c0, pts=pts, ob=ob, voff=voff, nkb=nkb: e.matmul(
                                psum[ob][:, c0:T], lhsT=V[:, kb, voff:voff + 128], rhs=PT[:, pts, c0:T],
                                start=(kb == 0), stop=(kb == nkb - 1)),
                               reads=[("V", kc_), ("PT", pts)], writes=[("ps", ob)])
                            if kb == nkb - 1:
                                if h % 2 == 0:
                                    orow, lrow = slice(0, 64), slice(64, 128)
                                else:
                                    orow, lrow = slice(64, 128), slice(0, 64)
                                rs = h % 2
                                op("dve", lambda e, ob=ob, orow=orow, lrow=lrow, rs=rs: e.reciprocal(out=rc[orow, rs, :], in_=psum[ob][lrow, :]),
                                   reads=[("ps", ob)], writes=[("rc", rs)])
                                op("dve", lambda e, ob=ob, orow=orow, rs=rs, m=m: e.tensor_tensor(out=yatt[orow, m, :], in0=psum[ob][orow, :],
                                                                                                 in1=rc[orow, rs, :], op=ALU.mult),
                                   reads=[("ps", ob), ("rc", rs)], writes=[("yatt", m, rs)])
                        for kind, oc in sched.get(idx, ()):
                            if kind == "a":
                                tail_a()
                            else:
                                tail_b(pending, oc)
                    pending = t0
            load_ynr(pending // T)
            tail_a()
            for oc in range(8):
                tail_b(pending, oc)
        P.barrier()

        T2 = 256
        NT2 = NTOK // T2
        with ExitStack() as ph:
            sb1 = lambda name, shape, dtp: ph.enter_context(nc.sbuf_tensor(name, shape, dtp))
            Wg = sb1("Wg", [128, 8, DFF], BF16)
            Wu = sb1("Wu", [128, 8, DFF], BF16)
            Wd = sb1("Wd", [128, 22, D], BF16)
            Wpg = sb1("Wpg", [128, 8, D], BF16)
            Wpp = sb1("Wpp", [128, 2, D], BF16)
            hh = sb1("hh", [128, 2, 8, T2], F32)
            sq2 = sb1("sq2", [128, 8, T2], BF16)
            hn = sb1("hn", [128, 2, 8, T2], BF16)
            hn2 = sb1("hn2", [128, 8, T2], BF16)
            NACT = 4
            actb = sb1("actb", [128, NACT, T2], BF16)
            NSG = 3
            sg = sb1("sg", [128, NSG, T2], BF16)
            pin = sb1("pin", [128, 2, T2], F32)
            pbf = sb1("pbf", [128, 2, T2], BF16)
            ee = sb1("ee", [128, 8, T2], F32)
            gth = sb1("gth", [128, 2, T2], F32)
            rs_f = sb1("rs_f", [128, T2], F32)
            rs_e = sb1("rs_e", [128, T2], F32)
            ln3 = sb1("ln3", [128, T2], F32)

            wg_v = wg_b.rearrange("(kc p) n -> p kc n", p=128)
            wu_v = wu_b.rearrange("(kc p) n -> p kc n", p=128)
            wd_v = wd_b.rearrange("(j p) n -> p j n", p=128)
            wpg_v = wpg_b.rearrange("(kc p) n -> p kc n", p=128)
            wpp_v = wpp_b.rearrange("(kc p) n -> p kc n", p=128)
            JG = [(0, 6), (6, 12), (12, 17), (17, 22)]

            def jgrp(j):
                for gi_, (a_, b_) in enumerate(JG):
                    if a_ <= j < b_:
                        return gi_

            for gi_, (ja, jb) in enumerate(JG):
                for (Wt, wv, nm, cv) in ((Wg, wg_v, "Wg", "cv_wg"), (Wu, wu_v, "Wu", "cv_wu")):
                    op("sp", lambda e, Wt=Wt, wv=wv, ja=ja, jb=jb: e.dma_start(out=Wt[:, :, ja * 128:jb * 128],
                                                                            in_=wv[:, :, ja * 128:jb * 128]),
                       reads=[cv], writes=[(nm, gi_)], dma=("w" + nm, gi_))
                op("sp", lambda e, ja=ja, jb=jb: e.dma_start(out=Wd[:, ja:jb, :], in_=wd_v[:, ja:jb, :]),
                   reads=["cv_wd"], writes=[("Wd", gi_)], dma=("wWd", gi_))
            op("sp", lambda e: e.dma_start(out=Wpp[:], in_=wpp_v[:, :, :]), reads=["cv_wpp"], writes=["Wpp"], dma="wWpp")
            op("sp", lambda e: e.dma_start(out=Wpg[:], in_=wpg_v[:, :, :]), reads=["cv_wpg"], writes=["Wpg"], dma="wWpg")

            h1_v = h1_s.rearrange("(kc p) t -> p kc t", p=128)
            out_v = outT.rearrange("(kc p) t -> p kc t", p=128)
            p_v = pT.rearrange("(kc p) t -> p kc t", p=128)
            sg_rot = [0]
            act_rot = [0]

            def HA(sl):
                return [("hh", sl, k_) for k_ in range(8)]

            def stage_a(i):
                t0 = i * T2
                sl = i % 2
                op("sp", lambda e: e.dma_start(out=hh[:, sl, :, :], in_=h1_v[:, :, t0:t0 + T2]),
                   reads=[("h1_all", k) for k in range(NXO)], writes=HA(sl), dma=("hh", sl))

            def stage_a1(i):
                sl = i % 2
                op("act", lambda e: e.activation(out=sq2[:], in_=hh[:, sl, :, :], func=AF.Square), reads=HA(sl), writes=["sq2"])

            def stage_a2(i):
                sl = i % 2
                H = ("hh", sl)
                norm_rstd([sq2[:, kc, :] for kc in range(8)], 1024.0, T2, rs_f, "rs_f", ln3, "sq2", 7)
                for kc in range(8):
                    op("dve", lambda e, kc=kc: e.scalar_tensor_tensor(out=hn[:, sl, kc, :], in0=hh[:, sl, kc, :], scalar=G("g_ffn", kc),
                                                                     in1=rs_f[:], op0=ALU.mult, op1=ALU.mult),
                       reads=[("hh", sl, kc), "gp", "rs_f"], writes=[("hn", sl, kc)])

            def stage_b(i):
                t0 = i * T2
                sl = i % 2
                H = ("hh", sl)
                pieces = []

                def b0():
                    op("sp", lambda e: e.dma_start(out=pin[:], in_=p_v[:, :, t0:t0 + T2]), writes=["pin"], dma="pin")
                    op("pool", lambda e: e.tensor_copy(out=pbf[:], in_=pin[:]), reads=["pin"], writes=["pbf"])
                    op("act", lambda e: e.activation(out=sq2[:], in_=hh[:, sl, :, :], func=AF.Square), reads=HA(sl), writes=["sq2"])
                pieces.append((0, b0))

                def b0b():
                    norm_rstd([sq2[:, kc, :] for kc in range(8)], 1024.0, T2, rs_f, "rs_f", ln3, "sq2", 7)
                    for kc in range(8):
                        op("dve", lambda e, kc=kc: e.scalar_tensor_tensor(out=hn2[:, kc, :], in0=hh[:, sl, kc, :], scalar=G("g_ple_in", kc),
                                                                         in1=rs_f[:], op0=ALU.mult, op1=ALU.mult),
                           reads=[("hh", sl, kc), "gp", "rs_f"], writes=[("hn2", kc)])
                pieces.append((3, b0b))

                def b1():
                    for oc2 in range(4):
                        b = 6
                        for half in range(2):
                            oc = oc2 * 2 + half
                            for kc in range(2):
                                op("pe", lambda e, kc=kc, oc=oc, half=half: e.matmul(psum[b][:, half * T2:(half + 1) * T2],
                                                                                    lhsT=Wpp[:, kc, oc * 128:(oc + 1) * 128], rhs=pbf[:, kc, :],
                                                                                    start=(kc == 0), stop=(kc == 1)),
                                   reads=["Wpp", "pbf"], writes=[("ps", b)])
                        op("act", lambda e, oc2=oc2: e.activation(out=ee[:, 2 * oc2:2 * oc2 + 2, :],
                                                                 in_=psum[b][:, :].rearrange("p (a t) -> p a t", a=2), func=AF.Copy),
                           reads=[("ps", b)], writes=[("ee", 2 * oc2), ("ee", 2 * oc2 + 1)])
                    op("act", lambda e: e.activation(out=sq2[:], in_=ee[:], func=AF.Square), reads=[("ee", k_) for k_ in range(8)], writes=["sq2"])
                pieces.append((4, b1))

                def b1b():
                    norm_rstd([sq2[:, kc, :] for kc in range(8)], 1024.0, T2, rs_e, "rs_e", ln3, "sq2", 7)
                pieces.append((6, b1b))

                def mk_gate(oc2):
                    def bg():
                        b = 6
                        for half in range(2):
                            oc = oc2 * 2 + half
                            for kc in range(8):
                                op("pe", lambda e, kc=kc, oc=oc, half=half: e.matmul(psum[b][:, half * T2:(half + 1) * T2],
                                                                                    lhsT=Wpg[:, kc, oc * 128:(oc + 1) * 128], rhs=hn2[:, kc, :],
                                                                                    start=(kc == 0), stop=(kc == 7)),
                                   reads=["Wpg", ("hn2", kc)], writes=[("ps", b)])
                        op("act", lambda e: e.activation(out=gth[:], in_=psum[b][:, :].rearrange("p (a t) -> p a t", a=2), func=AF.Tanh, scale=0.5),
                           reads=[("ps", b)], writes=["gth"])
                        for half in range(2):
                            oc = oc2 * 2 + half
                            op("dve", lambda e, oc=oc: e.scalar_tensor_tensor(out=ee[:, oc, :], in0=ee[:, oc, :], scalar=G("g_ple_post", oc),
                                                                             in1=rs_e[:], op0=ALU.mult, op1=ALU.mult),
                               reads=[("ee", oc), "gp", "rs_e"], writes=[("ee", oc)])
                            op("dve", lambda e, oc=oc, half=half: e.scalar_tensor_tensor(out=ee[:, oc, :], in0=gth[:, half, :], scalar=1.0,
                                                                                        in1=ee[:, oc, :], op0=ALU.add, op1=ALU.mult),
                               reads=[("ee", oc), "gth"], writes=[("ee", oc)])
                            op("dve", lambda e, oc=oc: e.scalar_tensor_tensor(out=hh[:, sl, oc, :], in0=ee[:, oc, :], scalar=0.5,
                                                                             in1=hh[:, sl, oc, :], op0=ALU.mult, op1=ALU.add),
                               reads=[("ee", oc), ("hh", sl, oc)], writes=[("hh", sl, oc)])
                    return bg
                for oc2 in range(4):
                    pieces.append((7 + oc2, mk_gate(oc2)))

                def bfa():
                    op("act", lambda e: e.activation(out=sq2[:], in_=hh[:, sl, :, :], func=AF.Square), reads=HA(sl), writes=["sq2"])
                pieces.append((11, bfa))

                def bf():
                    norm_rstd([sq2[:, kc, :] for kc in range(8)], 1024.0, T2, rs_f, "rs_f", ln3, "sq2", 7)
                    for kc in range(8):
                        op("dve", lambda e, kc=kc: e.scalar_tensor_tensor(out=hh[:, sl, kc, :], in0=hh[:, sl, kc, :], scalar=G("g_final", kc),
                                                                         in1=rs_f[:], op0=ALU.mult, op1=ALU.mult),
                           reads=[("hh", sl, kc), "gp", "rs_f"], writes=[("hh", sl, kc)])
                    op("sp", lambda e: e.dma_start(out=out_v[:, :, t0:t0 + T2], in_=hh[:, sl, :, :]), reads=HA(sl), dma=("out", sl))
                pieces.append((13, bf))
                return pieces

            def ffn_loop(i, sched):
                sl = i % 2
                H = ("hh", sl)
                acts = [None] * 22
                LD = 2
                for j in range(22 + LD):
                    if j < 22:
                        b = 4 + (j % 2)
                        for (Wt, nm, c0) in ((Wg, "Wg", 0), (Wu, "Wu", T2)):
                            for kc in range(8):
                                op("pe", lambda e, kc=kc, j=j, b=b, Wt=Wt, c0=c0: e.matmul(psum[b][:, c0:c0 + T2], lhsT=Wt[:, kc, j * 128:(j + 1) * 128],
                                                                                          rhs=hn[:, sl, kc, :], start=(kc == 0), stop=(kc == 7)),
                                   reads=[(nm, jgrp(j)), ("hn", sl, kc)], writes=[("ps", b)])
                        s_ = sg_rot[0]
                        sg_rot[0] = (s_ + 1) % NSG
                        a_ = act_rot[0]
                        act_rot[0] = (a_ + 1) % NACT
                        acts[j] = a_
                        op("act", lambda e, b=b, s_=s_: e.activation(out=sg[:, s_, :], in_=psum[b][:, 0:T2], func=AF.Silu),
                           reads=[("ps", b)], writes=[("sg", s_)])
                        op("dve", lambda e, b=b, s_=s_, a_=a_: e.tensor_tensor(out=actb[:, a_, :], in0=psum[b][:, T2:2 * T2], in1=sg[:, s_, :], op=ALU.mult),
                           reads=[("ps", b), ("sg", s_)], writes=[("act", a_)])
                    jj = j - LD
                    if jj >= 0:
                        a_ = acts[jj]
                        for oc in range(8):
                            db = oc // 2
                            half = oc % 2
                            op("pe", lambda e, jj=jj, oc=oc, db=db, half=half, a_=a_: e.matmul(
                                psum[db][:, half * T2:(half + 1) * T2], lhsT=Wd[:, jj, oc * 128:(oc + 1) * 128], rhs=actb[:, a_, :],
                                start=(jj == 0 and half == 0), stop=(jj == 21), skip_group_check=True),
                               reads=[("Wd", jgrp(jj)), ("act", a_)], writes=[("ps", db)])
                    for f in sched.get(j, ()):
                        f()
                for db in range(4):
                    op("dve", lambda e, db=db: e.tensor_tensor(out=hh[:, sl, 2 * db:2 * db + 2, :],
                                                              in0=psum[db][:, :].rearrange("p (a t) -> p a t", a=2),
                                                              in1=hh[:, sl, 2 * db:2 * db + 2, :], op=ALU.add),
                       reads=[("ps", db), ("hh", sl, 2 * db), ("hh", sl, 2 * db + 1)], writes=[("hh", sl, 2 * db), ("hh", sl, 2 * db + 1)])

            stage_a(0)
            stage_a1(0)
            stage_a2(0)
            prev_b = None
            for i in range(NT2):
                sched = {}
                if prev_b is not None:
                    for k, f in prev_b:
                        sched.setdefault(k, []).append(f)
                if i + 1 < NT2:
                    sched.setdefault(14, []).append(lambda i=i: stage_a(i + 1))
                    sched.setdefault(19, []).append(lambda i=i: stage_a1(i + 1))
                    sched.setdefault(21, []).append(lambda i=i: stage_a2(i + 1))
                ffn_loop(i, sched)
                prev_b = stage_b(i)
            for k, f in prev_b:
                f()

        P.emit()
    return nc


_NC_CACHE = {}


def _gpack(inp):
    gpk = np.zeros((128, NG), np.float32)

    def put(name, vec):
        v = np.asarray(vec, np.float32).reshape(-1)
        n = v.size // 128
        gpk[:, GC[name]:GC[name] + n] = v.reshape(n, 128).T

    put("g_mix", inp["g_mix"][0]); put("g_ffn", inp["g_ffn"][0]); put("g_ple_in", inp["g_ple_in"][0])
    put("g_ple_post", inp["g_ple_post"][0]); put("g_final", inp["g_final"])
    cw = np.asarray(inp["conv_w"][0], np.float32)
    for k in range(4):
        gpk[:, GC["conv_w"] + 4 * k:GC["conv_w"] + 4 * k + 4] = cw[k].reshape(4, 128).T
    put("conv_b", inp["conv_b"][0]); put("b_a", inp["b_rg_a"][0]); put("b_x", inp["b_rg_x"][0]); put("lru", inp["lru_L"][0])
    put("g_q", inp["g_q_lat"][0]); put("g_kv", inp["g_kv_lat"][0]); put("g_rnn", inp["g_out_rnn"][0]); put("g_att", inp["g_out_att"][0])
    inv = (10000.0 ** (-np.arange(0, 32, 2, dtype=np.float32) / 32)).astype(np.float32)
    gpk[:, GC["inv"]] = np.tile(np.concatenate([inv, inv]), 4)
    gpk[:, GC["sgn"]] = np.tile(np.concatenate([-np.ones(16, np.float32), np.ones(16, np.float32)]), 4)
    return gpk


def kernel(**inp):
    x = np.asarray(inp["x"], np.float32)
    p = np.asarray(inp["p"], np.float32)[0]
    positions = np.asarray(inp["positions"], np.int32)
    if "nc" not in _NC_CACHE:
        _NC_CACHE["nc"] = build_program()
    nc = _NC_CACHE["nc"]
    gpk = _gpack(inp)
    shared = {
        "gpk": gpk,
        "w_in": np.ascontiguousarray(inp["w_in"][0], np.float32),
        "w_rg_a": np.ascontiguousarray(inp["w_rg_a"][0], np.float32),
        "w_rg_x": np.ascontiguousarray(inp["w_rg_x"][0], np.float32),
        "w_q_up": np.ascontiguousarray(inp["w_q_up"][0], np.float32),
        "w_kv_up": np.ascontiguousarray(inp["w_kv_up"][0], np.float32),
        "w_out": np.ascontiguousarray(inp["w_out"][0], np.float32),
        "w_g": np.ascontiguousarray(inp["w_ffn_gate"][0], np.float32),
        "w_u": np.ascontiguousarray(inp["w_ffn_up"][0], np.float32),
        "w_d": np.ascontiguousarray(inp["w_ffn_down"][0], np.float32),
        "w_pg": np.ascontiguousarray(inp["w_ple_gate"][0], np.float32),
        "w_pp": np.ascontiguousarray(inp["w_ple_proj"][0], np.float32),
    }
    in_maps = []
    for c in range(NCORES):
        m = dict(shared)
        m["xT"] = np.ascontiguousarray(x[2 * c:2 * c + 2].reshape(NTOK, D).T)
        m["pT"] = np.ascontiguousarray(p[2 * c:2 * c + 2].reshape(NTOK, 256).T)
        m["pos"] = np.ascontiguousarray(positions[2 * c:2 * c + 2])
        in_maps.append(m)
    res = run_bass_kernel_spmd(nc, in_maps, core_ids=list(range(NCORES)))
    out = np.empty((16, SEQ, D), np.float32)
    for c in range(NCORES):
        out[2 * c:2 * c + 2] = np.asarray(res.results[c]["outT"]).T.reshape(2, SEQ, D)
    return out
```

```python
import math
from contextlib import ExitStack

import numpy as np
import concourse.bass as bass
import concourse.mybir as mybir
from concourse.bass_utils import run_bass_kernel_spmd

F32 = mybir.dt.float32
BF16 = mybir.dt.bfloat16
I32 = mybir.dt.int32
AF = mybir.ActivationFunctionType
ALU = mybir.AluOpType

NCORES = 8
SEQ = 4096
NTOK = 2 * SEQ
D = 1024
DFF = 2816
EPS = 1e-6
ATT_SCALE = 96 ** -0.5
MAGIC = 12582912.0
TWO_PI = 2.0 * math.pi
C1 = 6.28125
C2 = TWO_PI - C1

GC = {}
_c = 0
for _n, _w in (("g_mix", 8), ("g_ffn", 8), ("g_ple_in", 8), ("g_ple_post", 8), ("g_final", 8),
               ("conv_w", 16), ("conv_b", 4), ("b_a", 4), ("b_x", 4), ("lru", 4),
               ("g_q", 2), ("g_kv", 1), ("g_rnn", 4), ("g_att", 4), ("inv", 1), ("sgn", 1)):
    GC[_n] = _c
    _c += _w
NG = _c


class _Op:
    __slots__ = ("eng", "fn", "is_dma", "dkey", "waits", "signals", "sig_val", "pos")

    def __init__(self, eng, fn, is_dma, dkey):
        self.eng = eng
        self.fn = fn
        self.is_dma = is_dma
        self.dkey = dkey
        self.waits = []
        self.signals = False
        self.sig_val = None
        self.pos = 0


def deps_force(lst, eng):
    return [p for p in lst if p.is_dma or p.eng != eng]


class Prog:
    ENGS = ("pe", "dve", "act", "pool", "sp")

    def __init__(self, nc):
        self.nc = nc
        self.ops = {e: [] for e in self.ENGS}
        self.res = {}
        self.dma_counts = {}
        self.waited = {e: {} for e in self.ENGS}
        self.pending = {e: [] for e in self.ENGS}
        self.exempt = set()

    def _r(self, key):
        r = self.res.get(key)
        if r is None:
            r = self.res[key] = [None, {}]
        return r

    def barrier(self):
        lasts = []
        for e in self.ENGS:
            for o in reversed(self.ops[e]):
                if not o.is_dma:
                    lasts.append(o)
                    break
        dl = {}
        for e in self.ENGS:
            for o in self.ops[e]:
                if o.is_dma and o.dkey not in self.exempt:
                    dl[o.dkey] = o
        for e in self.ENGS:
            self.pending[e] = lasts + list(dl.values())

    def op(self, eng, fn, reads=(), writes=(), dma=None, marks=()):
        is_dma = dma is not None
        o = _Op(eng, fn, is_dma, dma)
        o.pos = len(self.ops[eng])
        deps = [(p, True) for p in self.pending[eng]]
        if deps:
            deps = [(p, True) for p in (self.pending[eng] if is_dma else deps_force(self.pending[eng], eng))]
            self.pending[eng] = []
        for k in reads:
            r = self._r(k)
            if r[0] is not None:
                deps.append((r[0], True))
        for k in writes:
            r = self._r(k)
            if r[0] is not None:
                deps.append((r[0], False))
            for rd in r[1].values():
                deps.append((rd, False))
        wd = self.waited[eng]
        for p, is_raw in deps:
            if p is o:
                continue
            if p.is_dma:
                key = ("d", p.dkey)
                if wd.get(key, 0) >= p.sig_val:
                    continue
                wd[key] = p.sig_val
                o.waits.append(p)
                continue
            if p.eng == eng and not is_dma:
                if eng == "pe":
                    continue
            key = ("e", p.eng)
            if wd.get(key, -1) >= p.pos:
                continue
            wd[key] = p.pos
            o.waits.append(p)
            p.signals = True
        for k in list(writes) + list(marks):
            r = self._r(k)
            r[0] = o
            r[1] = {}
        for k in reads:
            rk = ("dma", id(o)) if is_dma else eng
            self._r(k)[1][rk] = o
        if is_dma:
            c = self.dma_counts.get(dma, 0) + 1
            self.dma_counts[dma] = c
            o.sig_val = 16 * c
        self.ops[eng].append(o)
        return o

    def emit(self):
        nc = self.nc
        with ExitStack() as st:
            esem = {e: st.enter_context(nc.semaphore("s_" + e)) for e in self.ENGS}
            dsem = {k: st.enter_context(nc.semaphore("d_%s" % (k,))) for k in self.dma_counts}
            for e in self.ENGS:
                n = 0
                for o in self.ops[e]:
                    if not o.is_dma and o.signals:
                        n += 1
                        o.sig_val = n
            block = st.enter_context(nc.Block())

            def replay(engobj, e):
                for o in self.ops[e]:
                    for p in o.waits:
                        sem = dsem[p.dkey] if p.is_dma else esem[p.eng]
                        engobj.wait_ge(sem, p.sig_val)
                    ins = o.fn(engobj)
                    if o.is_dma:
                        ins.then_inc(dsem[o.dkey], 16)
                    elif o.signals:
                        ins.then_inc(esem[e], 1)
                if e == "sp":
                    for k, c in self.dma_counts.items():
                        engobj.wait_ge(dsem[k], 16 * c)

            @block.tensor
            def _(t):
                replay(t, "pe")

            @block.vector
            def _(v):
                replay(v, "dve")

            @block.scalar
            def _(s):
                replay(s, "act")

            @block.gpsimd
            def _(g):
                replay(g, "pool")

            @block.sync
            def _(s):
                replay(s, "sp")


def build_program(debug=False):
    nc = bass.Bass("TRN2", target_bir_lowering=False)
    SK = "ExternalOutput" if debug else "Internal"
    dt = nc.dram_tensor
    xT = dt("xT", [D, NTOK], F32, kind="ExternalInput").ap()
    pT = dt("pT", [256, NTOK], F32, kind="ExternalInput").ap()
    pos = dt("pos", [2, SEQ], I32, kind="ExternalInput").ap()
    gpk = dt("gpk", [128, NG], F32, kind="ExternalInput").ap()
    w_in = dt("w_in", [D, 1440], F32, kind="ExternalInput").ap()
    w_rg_a = dt("w_rg_a", [8, 64, 64], F32, kind="ExternalInput").ap()
    w_rg_x = dt("w_rg_x", [8, 64, 64], F32, kind="ExternalInput").ap()
    w_q_up = dt("w_q_up", [256, 768], F32, kind="ExternalInput").ap()
    w_kv_up = dt("w_kv_up", [128, 1024], F32, kind="ExternalInput").ap()
    w_out = dt("w_out", [D, D], F32, kind="ExternalInput").ap()
    w_g = dt("w_g", [D, DFF], F32, kind="ExternalInput").ap()
    w_u = dt("w_u", [D, DFF], F32, kind="ExternalInput").ap()
    w_d = dt("w_d", [DFF, D], F32, kind="ExternalInput").ap()
    w_pg = dt("w_pg", [D, D], F32, kind="ExternalInput").ap()
    w_pp = dt("w_pp", [256, D], F32, kind="ExternalInput").ap()
    outT = dt("outT", [D, NTOK], F32, kind="ExternalOutput").ap()
    yn_s = dt("yn_s", [512, NTOK], BF16, kind=SK).ap()
    q_s = dt("q_s", [96, 8, NTOK], BF16, kind=SK).ap()
    k_s = dt("k_s", [64, 8, NTOK], BF16, kind=SK).ap()
    kr_s = dt("kr_s", [32, NTOK], BF16, kind=SK).ap()
    v_s = dt("v_s", [128, NTOK // 128, 768], BF16, kind=SK).ap()
    h1_s = dt("h1_s", [D, NTOK], F32, kind=SK).ap()
    wg_b = dt("wg_b", [D, DFF], BF16, kind="Internal").ap()
    wu_b = dt("wu_b", [D, DFF], BF16, kind="Internal").ap()
    wd_b = dt("wd_b", [DFF, D], BF16, kind="Internal").ap()
    wpg_b = dt("wpg_b", [D, D], BF16, kind="Internal").ap()
    wpp_b = dt("wpp_b", [256, D], BF16, kind="Internal").ap()
    wout_b = dt("wout_b", [D, D], BF16, kind="Internal").ap()

    P = Prog(nc)
    op = P.op

    with ExitStack() as top:
        sb = lambda name, shape, dtp: top.enter_context(nc.sbuf_tensor(name, shape, dtp))
        ones = sb("ones", [128, 128], BF16)
        mask = sb("mask", [128, 128], BF16)
        gp = sb("gp", [128, NG], F32)
        cpar = sb("cpar", [128, 16], F32)
        psum = [top.enter_context(nc.psum_tensor("ps%d" % b, [128, 512], F32)) for b in range(8)]

        def G(name, j=0):
            c = GC[name] + j
            return gp[:, c:c + 1]

        op("sp", lambda e: e.dma_start(out=gp[:], in_=gpk[:, :]), writes=["gp"], dma="c0")
        op("dve", lambda e: e.memset(ones[:], 1.0), writes=["ones"])
        op("pool", lambda e: e.memset(mask[:], 1.0), writes=["mask"])
        op("pool", lambda e: e.affine_select(out=mask[:], in_=mask[:], pattern=[[1, 128]],
                                             compare_op=ALU.is_ge, fill=0.0, base=0,
                                             channel_multiplier=-1), reads=["mask"], writes=["mask"])
        L0 = GC["lru"]
        op("act", lambda e: e.activation(out=cpar[:, 0:4], in_=gp[:, L0:L0 + 4], func=AF.Exp, scale=-1.0),
           reads=["gp"], writes=["cpar"])
        op("act", lambda e: e.activation(out=cpar[:, 4:8], in_=cpar[:, 0:4], func=AF.Ln, bias=1.0),
           reads=["cpar"], writes=["cpar"])
        op("dve", lambda e: e.tensor_scalar(out=cpar[:, 0:4], in0=cpar[:, 4:8], scalar1=-8.0, scalar2=None,
                                            op0=ALU.mult), reads=["cpar"], writes=["cpar"])
        op("dve", lambda e: e.tensor_scalar(out=cpar[:, 4:8], in0=cpar[:, 0:4], scalar1=0.5, scalar2=None,
                                            op0=ALU.mult), reads=["cpar"], writes=["cpar"])
        Ba, Bx = GC["b_a"], GC["b_x"]
        op("dve", lambda e: e.tensor_scalar(out=cpar[:, 8:12], in0=gp[:, Ba:Ba + 4], scalar1=0.5, scalar2=None,
                                            op0=ALU.mult), reads=["gp", "cpar"], writes=["cpar"])
        op("dve", lambda e: e.tensor_scalar(out=cpar[:, 12:16], in0=gp[:, Bx:Bx + 4], scalar1=0.5, scalar2=None,
                                            op0=ALU.mult), reads=["gp", "cpar"], writes=["cpar"])

        bankrot = [0]

        def norm_rstd(chunks, dim, T, rstd, rstd_res, lnv, sq_res, bank):
            ps = psum[bank]
            n = len(chunks)
            for kc in range(n):
                op("pe", lambda e, kc=kc: e.matmul(ps[:, 0:T], lhsT=ones[:], rhs=chunks[kc],
                                                   start=(kc == 0), stop=(kc == n - 1)),
                   reads=["ones", sq_res], writes=[("ps", bank)])
            op("act", lambda e: e.activation(out=lnv[:, 0:T], in_=ps[:, 0:T], func=AF.Ln, scale=1.0 / dim, bias=EPS),
               reads=[("ps", bank)], writes=["lnv"])
            op("act", lambda e: e.activation(out=rstd[:, 0:T], in_=lnv[:, 0:T], func=AF.Exp, scale=-0.5),
               reads=["lnv"], writes=[rstd_res])

        T = 512
        NT1 = NTOK // T
        with ExitStack() as ph:
            sb1 = lambda name, shape, dtp: ph.enter_context(nc.sbuf_tensor(name, shape, dtp))
            Win = sb1("Win", [128, 8, 1440], BF16)
            Wkr = sb1("Wkr", [128, 8, 32], BF16)
            Wqn = sb1("Wqn", [128, 2, 8, 64], BF16)
            Wqr = sb1("Wqr", [128, 2, 2, 2, 4, 32], BF16)
            Wk = sb1("Wk", [128, 8, 64], BF16)
            Wv = sb1("Wv", [128, 8, 64], BF16)
            Wa = sb1("Wa", [128, 4, 128], BF16)
            Wx = sb1("Wx", [128, 4, 128], BF16)
            wstg = sb1("wstg", [128, 4, 128], F32)
            xs = sb1("xs", [128, 8, T], F32)
            sq = sb1("sq", [128, 8, T], BF16)
            xg = sb1("xg", [128, 8, T], BF16)
            rstd_m = sb1("rstd_m", [128, T], F32)
            rstd_x = sb1("rstd_x", [128, T], F32)
            lnv = sb1("lnv", [128, T], F32)
            xr = sb1("xr", [128, 4, T + 3], F32)
            hst = sb1("hst", [128, 4], F32)
            xc = sb1("xc", [128, 4, T], F32)
            xcb = sb1("xcb", [128, 4, T], BF16)
            thr = sb1("thr", [128, 4, T], F32)
            thi = sb1("thi", [128, 4, T], F32)
            av = sb1("av", [128, 4, T], F32)
            mu = sb1("mu", [128, 4, T], F32)
            hs = sb1("hs", [128, 4, T], F32)
            yr = sb1("yr", [128, 4, T], F32)
            yn = sb1("yn", [128, 4, T], BF16)
            cq = sb1("cq", [128, 2, T], F32)
            ckv = sb1("ckv", [128, T], F32)
            cqn = sb1("cqn", [128, 2, T], BF16)
            ckvn = sb1("ckvn", [128, T], BF16)
            Qst = sb1("Qst", [96, 8, T], BF16)
            Kst = sb1("Kst", [64, 8, T], BF16)
            Vst = sb1("Vst", [128, 4, 768], BF16)
            posi = sb1("posi", [128, T], I32)
            rp = sb1("rp", [128, 7, T], F32)
            rq = sb1("rq", [128, 2, T], F32)
            krf = sb1("krf", [32, T], BF16)
            sgq = sb1("sgq", [128, 1], F32)

            w_in_v = w_in.rearrange("(kc p) n -> p kc n", p=128)
            for kc in range(8):
                op("pool", lambda e, kc=kc: e.dma_start(out=Win[:, kc, :], in_=w_in_v[:, kc, :]),
                   writes=["Win"], dma="wWin")
                op("pool", lambda e, kc=kc: e.dma_start(out=Wkr[:, kc, 0:16], in_=w_in_v[:, kc, 1424:1440]),
                   writes=["Wkr"], dma="wWkr")
                op("pool", lambda e, kc=kc: e.dma_start(out=Wkr[:, kc, 16:32], in_=w_in_v[:, kc, 1408:1424]),
                   writes=["Wkr"], dma="wWkr")
            wq_v = w_q_up.rearrange("(kc p) (h d) -> p kc h d", p=128, d=96)
            for kc in range(2):
                op("pool", lambda e, kc=kc: e.dma_start(out=Wqn[:, kc, :, :], in_=wq_v[:, kc, :, 0:64]), writes=["Wqn"], dma="wWqn")
                for g in range(2):
                    hs_ = slice(4 * g, 4 * g + 4)
                    op("pool", lambda e, kc=kc, g=g, hs_=hs_: e.dma_start(out=Wqr[:, kc, g, 0, :, :], in_=wq_v[:, kc, hs_, 64:96]),
                       writes=["Wqr"], dma="wWqr")
                    op("pool", lambda e, kc=kc, g=g, hs_=hs_: e.dma_start(out=Wqr[:, kc, g, 1, :, 0:16], in_=wq_v[:, kc, hs_, 80:96]),
                       writes=["Wqr"], dma="wWqr")
                    op("pool", lambda e, kc=kc, g=g, hs_=hs_: e.dma_start(out=Wqr[:, kc, g, 1, :, 16:32], in_=wq_v[:, kc, hs_, 64:80]),
                       writes=["Wqr"], dma="wWqr")
            wkv_v = w_kv_up.rearrange("p (h e) -> p h e", e=128)
            op("pool", lambda e: e.dma_start(out=Wk[:, :, :], in_=wkv_v[:, :, 0:64]), writes=["Wk"], dma="wWk")
            op("pool", lambda e: e.dma_start(out=Wv[:, :, :], in_=wkv_v[:, :, 64:128]), writes=["Wv"], dma="wWv")
            for (Wt, wsrc, nm) in ((Wa, w_rg_a, "Wa"), (Wx, w_rg_x, "Wx")):
                op("dve", lambda e: e.memset(wstg[:], 0.0), writes=["wstg"])
                for ct in range(4):
                    op("sp", lambda e, ct=ct, wsrc=wsrc: e.dma_start(out=wstg[0:64, ct, 0:64], in_=wsrc[2 * ct, :, :]),
                       writes=["wstg"], dma="w2")
                    op("sp", lambda e, ct=ct, wsrc=wsrc: e.dma_start(out=wstg[64:128, ct, 64:128], in_=wsrc[2 * ct + 1, :, :]),
                       writes=["wstg"], dma="w2")
                op("dve", lambda e, Wt=Wt: e.tensor_copy(out=Wt[:], in_=wstg[:]), reads=["wstg"], writes=[nm])
            for tb in range(4):
                op("pool", lambda e, tb=tb: e.memset(Vst[:, tb, :].rearrange("p (m c) -> p m c", c=192)[:, :, 64:128], 1.0),
                   writes=["Vst"])
            cs_, ci_ = GC["sgn"], GC["inv"]
            op("dve", lambda e: e.tensor_scalar(out=sgq[:], in0=gp[:, cs_:cs_ + 1], scalar1=ATT_SCALE, scalar2=None, op0=ALU.mult),
               reads=["gp"], writes=["sgq"])

            def convert(src_ap, dst_ap, rows, cols, key):
                P.exempt.add(key)
                nsp = 1 if cols <= 2048 else 2
                cw = cols // nsp
                rstep = 256 if nsp == 2 else 512
                for r0 in range(0, rows, rstep):
                    r1 = min(rows, r0 + rstep)
                    sv = src_ap[r0:r1, :].rearrange("r (a b) -> r a b", b=cw)
                    dv = dst_ap[r0:r1, :].rearrange("r (a b) -> r a b", b=cw)
                    op("pool", lambda e, sv=sv, dv=dv: e.dma_start(out=dv, in_=sv), dma=key, marks=[key])
            def convert_all():
                convert(w_out, wout_b, D, D, "cv_wout")
                convert(w_g, wg_b, D, DFF, "cv_wg")
                convert(w_u, wu_b, D, DFF, "cv_wu")
                convert(w_d, wd_b, DFF, D, "cv_wd")
                convert(w_pp, wpp_b, 256, D, "cv_wpp")
                convert(w_pg, wpg_b, D, D, "cv_wpg")

            xT_v = xT.rearrange("(kc p) t -> p kc t", p=128)
            yn_v = yn_s.rearrange("(ct p) t -> p ct t", p=128)

            def load_x(i):
                t0 = i * T
                op("sp", lambda e: e.dma_start(out=xs[:], in_=xT_v[:, :, t0:t0 + T]), writes=["xs"], dma="xs")
                s = i // 8
                p0 = (i % 8) * T
                op("sp", lambda e: e.dma_start(out=posi[:], in_=pos[s:s + 1, p0:p0 + T].partition_broadcast(128)),
                   writes=["posi"], dma="pos")

            def proj(out_ps_ap, wsel, bank, nk=8, rd=("Win", "xg"), src=None):
                src = xg if src is None else src
                for kc in range(nk):
                    op("pe", lambda e, kc=kc: e.matmul(out_ps_ap, lhsT=wsel(kc), rhs=src[:, kc, :],
                                                       start=(kc == 0), stop=(kc == nk - 1)),
                       reads=[rd[0], (rd[1], kc)], writes=[("ps", bank)])

            def nb():
                b = bankrot[0]
                bankrot[0] = (b + 1) % 7
                return b

            TH, KK, RR, SIN, COS, SINQ, COSQ = [rp[:, j, :] for j in range(7)]
            CW, CB = GC["conv_w"], GC["conv_b"]
            def s0(i):
                first = (i % 8 == 0)
                op("act", lambda e: e.activation(out=sq[:], in_=xs[:], func=AF.Square), reads=["xs"], writes=["sq"])
                for kc in range(8):
                    op("act", lambda e, kc=kc: e.activation(out=xg[:, kc, :], in_=xs[:, kc, :], func=AF.Copy, scale=G("g_mix", kc)),
                       reads=["xs", "gp"], writes=[("xg", kc)])
                norm_rstd([sq[:, kc, :] for kc in range(8)], 1024.0, T, rstd_m, "rstd_m", lnv, "sq", 7)
                if first:
                    op("pool", lambda e: e.memset(xr[:, :, 0:3], 0.0), writes=["xr0", "xr1", "xr2", "xr3"])
                    op("pool", lambda e: e.memset(hst[:], 0.0), writes=[("hst", c_) for c_ in range(4)])

            load_x(0)
            R4 = range(4)
            s0(0)
            for i in range(NT1):
                t0 = i * T
                if i == 1:
                    convert_all()
                op("act", lambda e: e.activation(out=TH, in_=posi[:], func=AF.Copy, scale=gp[:, ci_:ci_ + 1]),
                   reads=["posi", "gp"], writes=["rp_th"])
                xb = []
                for ct in R4:
                    b = nb()
                    xb.append(b)
                    proj(psum[b][:, :], lambda kc, ct=ct: Win[:, kc, ct * 128:(ct + 1) * 128], b)
                op("dve", lambda e: e.tensor_scalar(out=KK, in0=TH, scalar1=1.0 / TWO_PI, scalar2=MAGIC, op0=ALU.mult, op1=ALU.add),
                   reads=["rp_th"], writes=["rp_k"])
                op("dve", lambda e: e.tensor_scalar(out=KK, in0=KK, scalar1=MAGIC, scalar2=None, op0=ALU.subtract),
                   reads=["rp_k"], writes=["rp_k"])
                op("dve", lambda e: e.scalar_tensor_tensor(out=RR, in0=KK, scalar=-C1, in1=TH, op0=ALU.mult, op1=ALU.add),
                   reads=["rp_k", "rp_th"], writes=["rp_r"])
                op("dve", lambda e: e.scalar_tensor_tensor(out=RR, in0=KK, scalar=-C2, in1=RR, op0=ALU.mult, op1=ALU.add),
                   reads=["rp_k", "rp_r"], writes=["rp_r"])
                op("dve", lambda e: e.tensor_scalar(out=RR, in0=RR, scalar1=3.1415925, scalar2=-3.1415925, op0=ALU.min, op1=ALU.max),
                   reads=["rp_r"], writes=["rp_r"])
                op("act", lambda e: e.activation(out=COS, in_=RR, func=AF.Sin), reads=["rp_r"], writes=["rp_cos"])
                for ct in R4:
                    b = xb[ct]
                    op("dve", lambda e, ct=ct, b=b: e.tensor_tensor(out=xr[:, ct, 3:3 + T], in0=psum[b][:, :], in1=rstd_m[:], op=ALU.mult),
                       reads=[("ps", b), "rstd_m"], writes=["xr%d" % ct])
                cqb = []
                for kc2 in range(2):
                    b = nb()
                    cqb.append(b)
                    proj(psum[b][:, :], lambda kc, kc2=kc2: Win[:, kc, 1024 + kc2 * 128:1024 + (kc2 + 1) * 128], b)
                bkv = nb()
                proj(psum[bkv][:, :], lambda kc: Win[:, kc, 1280:1408], bkv)
                for ct in R4:
                    op("dve", lambda e, ct=ct: e.tensor_scalar(out=xc[:, ct, :], in0=xr[:, ct, 3:3 + T],
                                                              scalar1=gp[:, CW + 12 + ct:CW + 13 + ct],
                                                              scalar2=gp[:, CB + ct:CB + ct + 1], op0=ALU.mult, op1=ALU.add),
                       reads=["xr%d" % ct, "gp"], writes=[("xc", ct)])
                for k in (2, 1, 0):
                    for ct in R4:
                        op("dve", lambda e, ct=ct, k=k: e.scalar_tensor_tensor(
                            out=xc[:, ct, :], in0=xr[:, ct, k:k + T], scalar=gp[:, CW + 4 * k + ct:CW + 4 * k + ct + 1], in1=xc[:, ct, :],
                            op0=ALU.mult, op1=ALU.add), reads=["xr%d" % ct, "gp", ("xc", ct)], writes=[("xc", ct)])
                for ct in R4:
                    op("pool", lambda e, ct=ct: e.tensor_copy(out=xr[:, ct, 0:3], in_=xr[:, ct, T:T + 3]), reads=["xr%d" % ct], writes=["xr%d" % ct])
                op("act", lambda e: e.activation(out=SIN, in_=COS, func=AF.Copy, scale=gp[:, cs_:cs_ + 1]),
                   reads=["rp_cos", "gp"], writes=["rp_sin"])
                op("act", lambda e: e.activation(out=SINQ, in_=COS, func=AF.Copy, scale=sgq[:]),
                   reads=["rp_cos", "sgq"], writes=["rp_sinq"])
                for ct in R4:
                    op("act", lambda e, ct=ct: e.activation(out=xcb[:, ct, :], in_=xc[:, ct, :], func=AF.Copy), reads=[("xc", ct)], writes=[("xcb", ct)])
                for ct in R4:
                    b1 = nb()
                    op("pe", lambda e, ct=ct, b1=b1: e.matmul(psum[b1][:, :], lhsT=Wa[:, ct, :], rhs=xcb[:, ct, :], start=True, stop=True),
                       reads=["Wa", ("xcb", ct)], writes=[("ps", b1)])
                    op("act", lambda e, ct=ct, b1=b1: e.activation(out=thr[:, ct, :], in_=psum[b1][:, :], func=AF.Tanh,
                                                                   scale=0.5, bias=cpar[:, 8 + ct:9 + ct]),
                       reads=[("ps", b1), "cpar"], writes=[("thr", ct)])
                for kc2 in range(2):
                    b = cqb[kc2]
                    op("dve", lambda e, kc2=kc2, b=b: e.tensor_tensor(out=cq[:, kc2, :], in0=psum[b][:, :], in1=rstd_m[:], op=ALU.mult),
                       reads=[("ps", b), "rstd_m"], writes=["cq"])
                op("dve", lambda e, b=bkv: e.tensor_tensor(out=ckv[:], in0=psum[b][:, :], in1=rstd_m[:], op=ALU.mult),
                   reads=[("ps", bkv), "rstd_m"], writes=["ckv"])
                for ct in R4:
                    b2 = nb()
                    op("pe", lambda e, ct=ct, b2=b2: e.matmul(psum[b2][:, :], lhsT=Wx[:, ct, :], rhs=xcb[:, ct, :], start=True, stop=True),
                       reads=["Wx", ("xcb", ct)], writes=[("ps", b2)])
                    op("act", lambda e, ct=ct, b2=b2: e.activation(out=thi[:, ct, :], in_=psum[b2][:, :], func=AF.Tanh,
                                                                   scale=0.5, bias=cpar[:, 12 + ct:13 + ct]),
                       reads=[("ps", b2), "cpar"], writes=[("thi", ct)])
                op("dve", lambda e: e.tensor_scalar(out=KK, in0=RR, scalar1=math.pi / 2, scalar2=None, op0=ALU.add),
                   reads=["rp_r", "rp_k"], writes=["rp_k"])
                op("dve", lambda e: e.tensor_scalar(out=RR, in0=KK, scalar1=math.pi, scalar2=-TWO_PI, op0=ALU.is_gt, op1=ALU.mult),
                   reads=["rp_k", "rp_r"], writes=["rp_r"])
                op("dve", lambda e: e.tensor_tensor(out=KK, in0=KK, in1=RR, op=ALU.add), reads=["rp_k", "rp_r"], writes=["rp_k"])
                op("dve", lambda e: e.tensor_scalar(out=KK, in0=KK, scalar1=3.1415925, scalar2=-3.1415925, op0=ALU.min, op1=ALU.max),
                   reads=["rp_k"], writes=["rp_k"])
                op("act", lambda e: e.activation(out=sq[:, 0:2, :], in_=cq[:], func=AF.Square), reads=["cq"], writes=["sq"])
                op("act", lambda e: e.activation(out=sq[:, 2, :], in_=ckv[:], func=AF.Square), reads=["ckv"], writes=["sq"])
                op("act", lambda e: e.activation(out=COS, in_=KK, func=AF.Sin), reads=["rp_k", "rp_sin", "rp_sinq"], writes=["rp_cos"])
                op("act", lambda e: e.activation(out=COSQ, in_=COS, func=AF.Copy, scale=ATT_SCALE), reads=["rp_cos"], writes=["rp_cosq"])
                bA = nb()
                proj(psum[bA][0:32, :], lambda kc: Win[:, kc, 1408:1440], bA)
                bB = nb()
                proj(psum[bB][0:32, :], lambda kc: Wkr[:, kc, :], bB, rd=("Wkr", "xg"))
                norm_rstd([sq[:, kc, :] for kc in range(2)], 256.0, T, rstd_x, "rstd_x", lnv, "sq", 7)
                for kc in range(2):
                    op("dve", lambda e, kc=kc: e.scalar_tensor_tensor(out=cqn[:, kc, :], in0=cq[:, kc, :], scalar=G("g_q", kc),
                                                                     in1=rstd_x[:], op0=ALU.mult, op1=ALU.mult),
                       reads=["cq", "gp", "rstd_x"], writes=[("cqn", kc)])
                norm_rstd([sq[:, 2, :]], 128.0, T, rstd_x, "rstd_x", lnv, "sq", 7)
                op("dve", lambda e: e.scalar_tensor_tensor(out=ckvn[:], in0=ckv[:], scalar=G("g_kv"), in1=rstd_x[:],
                                                           op0=ALU.mult, op1=ALU.mult),
                   reads=["ckv", "gp", "rstd_x"], writes=["ckvn"])
                for ct in R4:
                    op("act", lambda e, ct=ct: e.activation(out=av[:, ct, :], in_=thr[:, ct, :], func=AF.Exp,
                                                            scale=cpar[:, 4 + ct:5 + ct], bias=cpar[:, 4 + ct:5 + ct]),
                       reads=[("thr", ct), "cpar"], writes=[("av", ct)])
                for ct in R4:
                    op("act", lambda e, ct=ct: e.activation(out=mu[:, ct, :], in_=thr[:, ct, :], func=AF.Exp,
                                                            scale=cpar[:, ct:ct + 1], bias=cpar[:, ct:ct + 1]),
                       reads=[("thr", ct), "cpar"], writes=[("mu", ct)])
                gbk = []
                for ct in R4:
                    b = nb()
                    gbk.append(b)
                    proj(psum[b][:, :], lambda kc, ct=ct: Win[:, kc, 512 + ct * 128:512 + (ct + 1) * 128], b)
                if i + 1 < NT1:
                    load_x(i + 1)
                op("dve", lambda e, b=bA: e.tensor_tensor(out=rq[0:32, 0, :], in0=psum[b][0:32, :], in1=COS[0:32, :], op=ALU.mult),
                   reads=[("ps", bA), "rp_cos"], writes=["rq"])
                op("dve", lambda e, b=bB: e.tensor_tensor(out=rq[0:32, 1, :], in0=psum[b][0:32, :], in1=SIN[0:32, :], op=ALU.mult),
                   reads=[("ps", bB), "rp_sin", "rq"], writes=["rq"])
                op("dve", lambda e: e.tensor_tensor(out=rq[0:32, 0, :], in0=rq[0:32, 0, :], in1=rq[0:32, 1, :], op=ALU.add),
                   reads=["rq"], writes=["rq"])
                op("dve", lambda e: e.tensor_tensor(out=krf[:], in0=rq[0:32, 0, :], in1=rstd_m[0:32, :], op=ALU.mult),
                   reads=["rq", "rstd_m"], writes=["krf"])
                op("sp", lambda e, t0=t0: e.dma_start(out=kr_s[:, t0:t0 + T], in_=krf[:]), reads=["krf"], dma="stkr", marks=["kr_all"])
                for ct in R4:
                    b = gbk[ct]
                    op("dve", lambda e, ct=ct, b=b: e.tensor_tensor(out=yr[:, ct, :], in0=psum[b][:, :], in1=rstd_m[:], op=ALU.mult),
                       reads=[("ps", b), "rstd_m"], writes=["yr%d" % ct])
                for ct in R4:
                    op("dve", lambda e, ct=ct: e.scalar_tensor_tensor(out=thi[:, ct, :], in0=thi[:, ct, :], scalar=1.0, in1=xc[:, ct, :],
                                                                     op0=ALU.add, op1=ALU.mult),
                       reads=[("thi", ct), ("xc", ct)], writes=[("thi", ct)])
                qb = []
                for h in range(8):
                    b = nb()
                    proj(psum[b][0:64, :], lambda kc, h=h: Wqn[:, kc, h, :], b, nk=2, rd=("Wqn", "cqn"), src=cqn)
                    op("act", lambda e, h=h, b=b: e.activation(out=Qst[0:64, h, :], in_=psum[b][0:64, :], func=AF.Copy, scale=ATT_SCALE),
                       reads=[("ps", b)], writes=[("Qst", h, 0)])
                    if h == 3:
                        for ct in R4:
                            op("act", lambda e, ct=ct: e.activation(out=mu[:, ct, :], in_=mu[:, ct, :], func=AF.Ln, scale=-1.0, bias=1.0),
                               reads=[("mu", ct)], writes=[("mu", ct)])
                for ct in R4:
                    op("act", lambda e, ct=ct: e.activation(out=mu[:, ct, :], in_=mu[:, ct, :], func=AF.Exp, scale=0.5),
                       reads=[("mu", ct)], writes=[("mu", ct)])
                for g in range(2):
                    bA2 = nb()
                    proj(psum[bA2][:, :], lambda kc, g=g: Wqr[:, kc, g, 0, :, :].rearrange("p h d -> p (h d)"), bA2, nk=2, rd=("Wqr", "cqn"), src=cqn)
                    op("dve", lambda e, bA2=bA2: e.tensor_tensor(out=rq[:, 0, :], in0=psum[bA2][:, :], in1=COSQ, op=ALU.mult),
                       reads=[("ps", bA2), "rp_cosq", "rq"], writes=["rq"])
                    bB2 = nb()
                    proj(psum[bB2][:, :], lambda kc, g=g: Wqr[:, kc, g, 1, :, :].rearrange("p h d -> p (h d)"), bB2, nk=2, rd=("Wqr", "cqn"), src=cqn)
                    op("dve", lambda e, bB2=bB2: e.tensor_tensor(out=rq[:, 1, :], in0=psum[bB2][:, :], in1=SINQ, op=ALU.mult),
                       reads=[("ps", bB2), "rp_sinq", "rq"], writes=["rq"])
                    for hh in range(4):
                        h = 4 * g + hh
                        op("dve", lambda e, h=h, hh=hh: e.tensor_tensor(out=Qst[64:96, h, :], in0=rq[32 * hh:32 * hh + 32, 0, :],
                                                                         in1=rq[32 * hh:32 * hh + 32, 1, :], op=ALU.add),
                           reads=["rq"], writes=[("Qst", h, 1)])
                op("sp", lambda e, t0=t0: e.dma_start(out=q_s[:, :, t0:t0 + T], in_=Qst[:]), reads=[("Qst", h_, z_) for h_ in range(8) for z_ in range(2)], dma="stq", marks=["q_all"])
                for ct in R4:
                    op("dve", lambda e, ct=ct: e.scalar_tensor_tensor(out=thi[:, ct, :], in0=thi[:, ct, :], scalar=0.5, in1=mu[:, ct, :],
                                                                     op0=ALU.mult, op1=ALU.mult),
                       reads=[("thi", ct), ("mu", ct)], writes=[("thi", ct)])
                for ct in R4:
                    op("dve", lambda e, ct=ct: e.tensor_tensor_scan(out=hs[:, ct, :], data0=av[:, ct, :], data1=thi[:, ct, :],
                                                                   initial=hst[:, ct:ct + 1], op0=ALU.mult, op1=ALU.add),
                       reads=[("av", ct), ("thi", ct), ("hst", ct)], writes=[("hs", ct)])
                    op("pool", lambda e, ct=ct: e.tensor_copy(out=hst[:, ct:ct + 1], in_=hs[:, ct, T - 1:T]),
                       reads=[("hs", ct)], writes=[("hst", ct)])
                if i + 1 < NT1:
                    s0(i + 1)
                for h in range(8):
                    b = nb()
                    op("pe", lambda e, h=h, b=b: e.matmul(psum[b][0:64, :], lhsT=Wk[:, h, :], rhs=ckvn[:], start=True, stop=True),
                       reads=["Wk", "ckvn"], writes=[("ps", b)])
                    op("act", lambda e, h=h, b=b: e.activation(out=Kst[:, h, :], in_=psum[b][0:64, :], func=AF.Copy),
                       reads=[("ps", b)], writes=[("Kst", h)])
                op("sp", lambda e, t0=t0: e.dma_start(out=k_s[:, :, t0:t0 + T], in_=Kst[:]), reads=[("Kst", h_) for h_ in range(8)], dma="stk", marks=["k_all"])
                for ct in R4:
                    op("act", lambda e, ct=ct: e.activation(out=yr[:, ct, :], in_=yr[:, ct, :], func=AF.Gelu_apprx_tanh),
                       reads=["yr%d" % ct], writes=["yr%d" % ct])
                for tb in range(4):
                    b = nb()
                    op("pe", lambda e, tb=tb, b=b: e.matmul(psum[b][:, :], lhsT=ckvn[:, tb * 128:(tb + 1) * 128],
                                                            rhs=Wv[:].rearrange("p h d -> p (h d)"), start=True, stop=True),
                       reads=["Wv", "ckvn"], writes=[("ps", b)])
                    for par in range(2):
                        c0 = 0 if par == 0 else 128
                        op("dve", lambda e, tb=tb, b=b, par=par, c0=c0: e.tensor_copy(
                            out=Vst[:, tb, :].rearrange("p (m c) -> p m c", c=192)[:, :, c0:c0 + 64],
                            in_=psum[b][:, :].rearrange("p (m t d) -> p m t d", t=2, d=64)[:, :, par, :]),
                           reads=[("ps", b)], writes=[("Vst", tb, par)])
                op("sp", lambda e, i=i: e.dma_start(out=v_s[:, i * 4:(i + 1) * 4, :], in_=Vst[:]), reads=["Vst"] + [("Vst", a_, b_) for a_ in range(4) for b_ in range(2)], dma="stv", marks=["v_all"])
                for ct in R4:
                    op("dve", lambda e, ct=ct: e.tensor_tensor(out=yr[:, ct, :], in0=hs[:, ct, :], in1=yr[:, ct, :], op=ALU.mult),
                       reads=[("hs", ct), "yr%d" % ct], writes=["yr%d" % ct])
                op("act", lambda e: e.activation(out=sq[:, 4:8, :], in_=yr[:], func=AF.Square), reads=["yr0", "yr1", "yr2", "yr3"], writes=["sq"])
                norm_rstd([sq[:, 4 + c, :] for c in range(4)], 512.0, T, rstd_x, "rstd_x", lnv, "sq", 7)
                for ct in R4:
                    op("dve", lambda e, ct=ct: e.scalar_tensor_tensor(out=yn[:, ct, :], in0=yr[:, ct, :], scalar=G("g_rnn", ct),
                                                                     in1=rstd_x[:], op0=ALU.mult, op1=ALU.mult),
                       reads=["yr%d" % ct, "gp", "rstd_x"], writes=[("yn", ct)])
                op("sp", lambda e, t0=t0: e.dma_start(out=yn_v[:, :, t0:t0 + T], in_=yn[:]), reads=[("yn", c_) for c_ in range(4)], dma="sty", marks=["y_all"])
        P.barrier()

        with ExitStack() as ph:
            sb1 = lambda name, shape, dtp: ph.enter_context(nc.sbuf_tensor(name, shape, dtp))
            KT = sb1("KT", [96, 8, SEQ], BF16)
            V = sb1("V", [128, 32, 768], BF16)
            Wout = sb1("Wout", [128, 8, D], BF16)
            Qt = sb1("Qt", [96, 2, 8, T], BF16)
            NPT = 6
            PT = sb1("PT", [128, NPT, T], BF16)
            yatt = sb1("yatt", [128, 4, T], F32)
            rc = sb1("rc", [128, 2, T], F32)
            yan = sb1("yan", [128, 4, T], BF16)
            ynr = sb1("ynr", [128, 4, T], BF16)
            NXO = 4
            xo = sb1("xo", [128, NXO, T], F32)
            sqa = sb1("sqa", [128, 4, T], BF16)
            rstd_a = sb1("rstd_a", [128, T], F32)
            lnv2 = sb1("lnv2", [128, T], F32)

            w_out_v = wout_b.rearrange("(kc p) n -> p kc n", p=128)
            op("sp", lambda e: e.dma_start(out=Wout[:], in_=w_out_v[:, :, :]), reads=["cv_wout"], writes=["Wout"], dma="wWout")
            xo_rot = [0]
            pt_rot = [0]
            s_rot = [0]
            LA = 3

            def tail_a():
                op("act", lambda e: e.activation(out=sqa[:], in_=yatt[:], func=AF.Square), reads=[("yatt", m_, z_) for m_ in range(4) for z_ in range(2)], writes=["sqa"])
                norm_rstd([sqa[:, c, :] for c in range(4)], 512.0, T, rstd_a, "rstd_a", lnv2, "sqa", 6)
                for m in range(4):
                    op("dve", lambda e, m=m: e.scalar_tensor_tensor(out=yan[:, m, :], in0=yatt[:, m, :], scalar=G("g_att", m),
                                                                   in1=rstd_a[:], op0=ALU.mult, op1=ALU.mult),
                       reads=[("yatt", m, 0), ("yatt", m, 1), "gp", "rstd_a"], writes=[("yan", m)])

            def tail_b(t0, oc):
                xsl = xo_rot[0]
                xo_rot[0] = (xsl + 1) % NXO
                pb = 6 + (oc % 2)
                op("sp", lambda e: e.dma_start(out=xo[:, xsl, :], in_=xT[oc * 128:(oc + 1) * 128, t0:t0 + T]),
                   writes=[("xo", xsl)], dma=("xo", xsl))
                for kc in range(8):
                    rhs = ynr[:, kc, :] if kc < 4 else yan[:, kc - 4, :]
                    op("pe", lambda e, kc=kc, rhs=rhs: e.matmul(psum[pb][:, :], lhsT=Wout[:, kc, oc * 128:(oc + 1) * 128],
                                                               rhs=rhs, start=(kc == 0), stop=(kc == 7)),
                       reads=["Wout", ("ynr" if kc < 4 else ("yan", kc - 4))], writes=[("ps", pb)])
                op("dve", lambda e: e.tensor_tensor(out=xo[:, xsl, :], in0=psum[pb][:, :], in1=xo[:, xsl, :], op=ALU.add),
                   reads=[("ps", pb), ("xo", xsl)], writes=[("xo", xsl)])
                op("sp", lambda e: e.dma_start(out=h1_s[oc * 128:(oc + 1) * 128, t0:t0 + T], in_=xo[:, xsl, :]),
                   reads=[("xo", xsl)], dma=("xst", xsl), marks=[("h1_all", xsl)])

            def load_q(gi):
                t0 = gi * T
                qsl = gi % 2
                op("sp", lambda e: e.dma_start(out=Qt[:, qsl, :, :], in_=q_s[:, :, t0:t0 + T]), reads=["q_all"], writes=[("Qt", qsl)], dma=("qt", qsl))

            def load_ynr(gi):
                t0 = gi * T
                op("sp", lambda e: e.dma_start(out=ynr[:], in_=yn_v[:, :, t0:t0 + T]), reads=["y_all"], writes=["ynr"], dma="ynr")

            pending = None
            load_q(0)
            for s in range(2):
                base = s * SEQ
                for c in range(8):
                    op("sp", lambda e, c=c, base=base: e.dma_start(out=KT[0:64, :, c * T:(c + 1) * T],
                                                                   in_=k_s[:, :, base + c * T:base + (c + 1) * T]),
                       reads=["k_all"], writes=[("KT", c)], dma=("kt", c))
                    for h in range(8):
                        op("sp", lambda e, c=c, base=base, h=h: e.dma_start(out=KT[64:96, h, c * T:(c + 1) * T],
                                                                            in_=kr_s[:, base + c * T:base + (c + 1) * T]),
                           reads=["kr_all"], writes=([("KTr", c)] if h == 0 else []), marks=([("KTr", c)] if h > 0 else []),
                           dma=("ktr", c))
                    kb0 = base // 128 + c * 4
                    op("sp", lambda e, c=c, kb0=kb0: e.dma_start(out=V[:, c * 4:(c + 1) * 4, :], in_=v_s[:, kb0:kb0 + 4, :]),
                       reads=["v_all"], writes=[("V", c)], dma=("vt", c))
                for i in range(8):
                    gi = s * 8 + i
                    t0 = base + i * T
                    qsl = gi % 2
                    if gi + 1 < 16:
                        load_q(gi + 1)
                    if pending is not None:
                        load_ynr(pending // T)
                    nkb = 4 * (i + 1)
                    blocks = [(h, kb) for h in range(8) for kb in range(nkb)]
                    nblk = len(blocks)
                    info = [None] * nblk
                    sched = {}
                    if pending is not None:
                        sched[3] = [("a", None)]
                        for oc in range(8):
                            sched.setdefault(6 + 2 * oc, []).append(("b", oc))
                    for idx in range(nblk + LA):
                        if idx < nblk:
                            h, kb = blocks[idx]
                            r = kb - 4 * i
                            c0 = 128 * r if r > 0 else 0
                            sbk = s_rot[0]
                            s_rot[0] = (sbk + 1) % 4
                            pts = pt_rot[0]
                            pt_rot[0] = (pts + 1) % NPT
                            info[idx] = (sbk, pts, c0)
                            kc_ = kb // 4
                            op("pe", lambda e, h=h, kb=kb, c0=c0, sbk=sbk, qsl=qsl: e.matmul(
                                psum[sbk][:, c0:T], lhsT=KT[:, h, kb * 128:(kb + 1) * 128], rhs=Qt[:, qsl, h, c0:T], start=True, stop=True),
                               reads=[("KT", kc_), ("KTr", kc_), ("Qt", qsl)], writes=[("ps", sbk)])
                            op("act", lambda e, c0=c0, sbk=sbk, pts=pts: e.activation(out=PT[:, pts, c0:T], in_=psum[sbk][:, c0:T], func=AF.Exp),
                               reads=[("ps", sbk)], writes=[("PT", pts)])
                            if r >= 0:
                                op("pool", lambda e, c0=c0, pts=pts: e.tensor_tensor(out=PT[:, pts, c0:c0 + 128], in0=PT[:, pts, c0:c0 + 128],
                                                                                     in1=mask[:], op=ALU.mult),
                                   reads=[("PT", pts), "mask"], writes=[("PT", pts)])
                        j = idx - LA
                        if j >= 0:
                            h, kb = blocks[j]
                            sbk, pts, c0 = info[j]
                            ob = 4 + (h % 2)
                            m = h // 2
                            voff = m * 192 + (0 if h % 2 == 0 else 64)
                            kc_ = kb // 4
                            op("pe", lambda e, kb=kb, c0=c0, pts=pts, ob=ob, voff=voff, nkb=nkb: e.matmul(
                                psum[ob][:, c0:T], lhsT=V[:, kb, voff:voff + 128], rhs=PT[:, pts, c0:T],
                                start=(kb == 0), stop=(kb == nkb - 1)),
                               reads=[("V", kc_), ("PT", pts)], writes=[("ps", ob)])
                            if kb == nkb - 1:
                                if h % 2 == 0:
                                    orow, lrow = slice(0, 64), slice(64, 128)
                                else:
                                    orow, lrow = slice(64, 128), slice(0, 64)
                                rs = h % 2
                                op("dve", lambda e, ob=ob, orow=orow, lrow=lrow, rs=rs: e.reciprocal(out=rc[orow, rs, :], in_=psum[ob][lrow, :]),
                                   reads=[("ps", ob)], writes=[("rc", rs)])
                                op("dve", lambda e, ob=ob, orow=orow, rs=rs, m=m: e.tensor_tensor(out=yatt[orow, m, :], in0=psum[ob][orow, :],
                                                                                                 in1=rc[orow, rs, :], op=ALU.mult),
                                   reads=[("ps", ob), ("rc", rs)], writes=[("yatt", m, rs)])
                        for kind, oc in sched.get(idx, ()):
                            if kind == "a":
                                tail_a()
                            else:
                                tail_b(pending, oc)
                    pending = t0
            load_ynr(pending // T)
            tail_a()
            for oc in range(8):
                tail_b(pending, oc)
        P.barrier()

        T2 = 256
        NT2 = NTOK // T2
        with ExitStack() as ph:
            sb1 = lambda name, shape, dtp: ph.enter_context(nc.sbuf_tensor(name, shape, dtp))
            Wg = sb1("Wg", [128, 8, DFF], BF16)
            Wu = sb1("Wu", [128, 8, DFF], BF16)
            Wd = sb1("Wd", [128, 22, D], BF16)
            Wpg = sb1("Wpg", [128, 8, D], BF16)
            Wpp = sb1("Wpp", [128, 2, D], BF16)
            hh = sb1("hh", [128, 2, 8, T2], F32)
            sq2 = sb1("sq2", [128, 8, T2], BF16)
            hn = sb1("hn", [128, 2, 8, T2], BF16)
            hn2 = sb1("hn2", [128, 8, T2], BF16)
            NACT = 4
            actb = sb1("actb", [128, NACT, T2], BF16)
            NSG = 3
            sg = sb1("sg", [128, NSG, T2], BF16)
            pin = sb1("pin", [128, 2, T2], F32)
            pbf = sb1("pbf", [128, 2, T2], BF16)
            ee = sb1("ee", [128, 8, T2], F32)
            gth = sb1("gth", [128, 2, T2], F32)
            rs_f = sb1("rs_f", [128, T2], F32)
            rs_e = sb1("rs_e", [128, T2], F32)
            ln3 = sb1("ln3", [128, T2], F32)

            wg_v = wg_b.rearrange("(kc p) n -> p kc n", p=128)
            wu_v = wu_b.rearrange("(kc p) n -> p kc n", p=128)
            wd_v = wd_b.rearrange("(j p) n -> p j n", p=128)
            wpg_v = wpg_b.rearrange("(kc p) n -> p kc n", p=128)
            wpp_v = wpp_b.rearrange("(kc p) n -> p kc n", p=128)
            JG = [(0, 6), (6, 12), (12, 17), (17, 22)]

            def jgrp(j):
                for gi_, (a_, b_) in enumerate(JG):
                    if a_ <= j < b_:
                        return gi_

            for gi_, (ja, jb) in enumerate(JG):
                for (Wt, wv, nm, cv) in ((Wg, wg_v, "Wg", "cv_wg"), (Wu, wu_v, "Wu", "cv_wu")):
                    op("sp", lambda e, Wt=Wt, wv=wv, ja=ja, jb=jb: e.dma_start(out=Wt[:, :, ja * 128:jb * 128],
                                                                            in_=wv[:, :, ja * 128:jb * 128]),
                       reads=[cv], writes=[(nm, gi_)], dma=("w" + nm, gi_))
                op("sp", lambda e, ja=ja, jb=jb: e.dma_start(out=Wd[:, ja:jb, :], in_=wd_v[:, ja:jb, :]),
                   reads=["cv_wd"], writes=[("Wd", gi_)], dma=("wWd", gi_))
            op("sp", lambda e: e.dma_start(out=Wpp[:], in_=wpp_v[:, :, :]), reads=["cv_wpp"], writes=["Wpp"], dma="wWpp")
            op("sp", lambda e: e.dma_start(out=Wpg[:], in_=wpg_v[:, :, :]), reads=["cv_wpg"], writes=["Wpg"], dma="wWpg")

            h1_v = h1_s.rearrange("(kc p) t -> p kc t", p=128)
            out_v = outT.rearrange("(kc p) t -> p kc t", p=128)
            p_v = pT.rearrange("(kc p) t -> p kc t", p=128)
            sg_rot = [0]
            act_rot = [0]

            def HA(sl):
                return [("hh", sl, k_) for k_ in range(8)]

            def stage_a(i):
                t0 = i * T2
                sl = i % 2
                op("sp", lambda e: e.dma_start(out=hh[:, sl, :, :], in_=h1_v[:, :, t0:t0 + T2]),
                   reads=[("h1_all", k) for k in range(NXO)], writes=HA(sl), dma=("hh", sl))

            def stage_a1(i):
                sl = i % 2
                op("act", lambda e: e.activation(out=sq2[:], in_=hh[:, sl, :, :], func=AF.Square), reads=HA(sl), writes=["sq2"])

            def stage_a2(i):
                sl = i % 2
                H = ("hh", sl)
                norm_rstd([sq2[:, kc, :] for kc in range(8)], 1024.0, T2, rs_f, "rs_f", ln3, "sq2", 7)
                for kc in range(8):
                    op("dve", lambda e, kc=kc: e.scalar_tensor_tensor(out=hn[:, sl, kc, :], in0=hh[:, sl, kc, :], scalar=G("g_ffn", kc),
                                                                     in1=rs_f[:], op0=ALU.mult, op1=ALU.mult),
                       reads=[("hh", sl, kc), "gp", "rs_f"], writes=[("hn", sl, kc)])

            def pin_load(i):
                t0 = i * T2
                op("sp", lambda e: e.dma_start(out=pin[:], in_=p_v[:, :, t0:t0 + T2]), writes=["pin"], dma="pin")
                op("pool", lambda e: e.tensor_copy(out=pbf[:], in_=pin[:]), reads=["pin"], writes=["pbf"])

            def stage_b(i):
                t0 = i * T2
                sl = i % 2
                H = ("hh", sl)
                pieces = []

                def b0():
                    op("act", lambda e: e.activation(out=sq2[:], in_=hh[:, sl, :, :], func=AF.Square), reads=HA(sl), writes=["sq2"])
                pieces.append((0, b0))

                def b0b():
                    norm_rstd([sq2[:, kc, :] for kc in range(8)], 1024.0, T2, rs_f, "rs_f", ln3, "sq2", 7)
                    for kc in range(8):
                        op("dve", lambda e, kc=kc: e.scalar_tensor_tensor(out=hn2[:, kc, :], in0=hh[:, sl, kc, :], scalar=G("g_ple_in", kc),
                                                                         in1=rs_f[:], op0=ALU.mult, op1=ALU.mult),
                           reads=[("hh", sl, kc), "gp", "rs_f"], writes=[("hn2", kc)])
                pieces.append((3, b0b))

                def b1():
                    for oc2 in range(4):
                        b = 6
                        for half in range(2):
                            oc = oc2 * 2 + half
                            for kc in range(2):
                                op("pe", lambda e, kc=kc, oc=oc, half=half: e.matmul(psum[b][:, half * T2:(half + 1) * T2],
                                                                                    lhsT=Wpp[:, kc, oc * 128:(oc + 1) * 128], rhs=pbf[:, kc, :],
                                                                                    start=(kc == 0), stop=(kc == 1)),
                                   reads=["Wpp", "pbf"], writes=[("ps", b)])
                        op("act", lambda e, oc2=oc2: e.activation(out=ee[:, 2 * oc2:2 * oc2 + 2, :],
                                                                 in_=psum[b][:, :].rearrange("p (a t) -> p a t", a=2), func=AF.Copy),
                           reads=[("ps", b)], writes=[("ee", 2 * oc2), ("ee", 2 * oc2 + 1)])
                    op("act", lambda e: e.activation(out=sq2[:], in_=ee[:], func=AF.Square), reads=[("ee", k_) for k_ in range(8)], writes=["sq2"])
                pieces.append((4, b1))

                def b1b():
                    norm_rstd([sq2[:, kc, :] for kc in range(8)], 1024.0, T2, rs_e, "rs_e", ln3, "sq2", 7)
                pieces.append((6, b1b))

                def mk_gate(oc2):
                    def bg():
                        b = 6
                        for half in range(2):
                            oc = oc2 * 2 + half
                            for kc in range(8):
                                op("pe", lambda e, kc=kc, oc=oc, half=half: e.matmul(psum[b][:, half * T2:(half + 1) * T2],
                                                                                    lhsT=Wpg[:, kc, oc * 128:(oc + 1) * 128], rhs=hn2[:, kc, :],
                                                                                    start=(kc == 0), stop=(kc == 7)),
                                   reads=["Wpg", ("hn2", kc)], writes=[("ps", b)])
                        op("act", lambda e: e.activation(out=gth[:], in_=psum[b][:, :].rearrange("p (a t) -> p a t", a=2), func=AF.Tanh, scale=0.5),
                           reads=[("ps", b)], writes=["gth"])
                        for half in range(2):
                            oc = oc2 * 2 + half
                            op("dve", lambda e, oc=oc: e.scalar_tensor_tensor(out=ee[:, oc, :], in0=ee[:, oc, :], scalar=G("g_ple_post", oc),
                                                                             in1=rs_e[:], op0=ALU.mult, op1=ALU.mult),
                               reads=[("ee", oc), "gp", "rs_e"], writes=[("ee", oc)])
                            op("dve", lambda e, oc=oc, half=half: e.scalar_tensor_tensor(out=ee[:, oc, :], in0=gth[:, half, :], scalar=1.0,
                                                                                        in1=ee[:, oc, :], op0=ALU.add, op1=ALU.mult),
                               reads=[("ee", oc), "gth"], writes=[("ee", oc)])
                            op("dve", lambda e, oc=oc: e.scalar_tensor_tensor(out=hh[:, sl, oc, :], in0=ee[:, oc, :], scalar=0.5,
                                                                             in1=hh[:, sl, oc, :], op0=ALU.mult, op1=ALU.add),
                               reads=[("ee", oc), ("hh", sl, oc)], writes=[("hh", sl, oc)])
                    return bg
                for oc2 in range(4):
                    pieces.append((7 + oc2, mk_gate(oc2)))

                def bfa():
                    op("act", lambda e: e.activation(out=sq2[:], in_=hh[:, sl, :, :], func=AF.Square), reads=HA(sl), writes=["sq2"])
                pieces.append((11, bfa))

                def bf():
                    norm_rstd([sq2[:, kc, :] for kc in range(8)], 1024.0, T2, rs_f, "rs_f", ln3, "sq2", 7)
                    for kc in range(8):
                        op("dve", lambda e, kc=kc: e.scalar_tensor_tensor(out=hh[:, sl, kc, :], in0=hh[:, sl, kc, :], scalar=G("g_final", kc),
                                                                         in1=rs_f[:], op0=ALU.mult, op1=ALU.mult),
                           reads=[("hh", sl, kc), "gp", "rs_f"], writes=[("hh", sl, kc)])
                    op("sp", lambda e: e.dma_start(out=out_v[:, :, t0:t0 + T2], in_=hh[:, sl, :, :]), reads=HA(sl), dma=("out", sl))
                pieces.append((13, bf))
                return pieces

            def ffn_loop(i, sched):
                sl = i % 2
                H = ("hh", sl)
                acts = [None] * 22
                LD = 2
                for j in range(22 + LD):
                    if j < 22:
                        b = 4 + (j % 2)
                        for (Wt, nm, c0) in ((Wg, "Wg", 0), (Wu, "Wu", T2)):
                            for kc in range(8):
                                op("pe", lambda e, kc=kc, j=j, b=b, Wt=Wt, c0=c0: e.matmul(psum[b][:, c0:c0 + T2], lhsT=Wt[:, kc, j * 128:(j + 1) * 128],
                                                                                          rhs=hn[:, sl, kc, :], start=(kc == 0), stop=(kc == 7)),
                                   reads=[(nm, jgrp(j)), ("hn", sl, kc)], writes=[("ps", b)])
                        s_ = sg_rot[0]
                        sg_rot[0] = (s_ + 1) % NSG
                        a_ = act_rot[0]
                        act_rot[0] = (a_ + 1) % NACT
                        acts[j] = a_
                        op("act", lambda e, b=b, s_=s_: e.activation(out=sg[:, s_, :], in_=psum[b][:, 0:T2], func=AF.Silu),
                           reads=[("ps", b)], writes=[("sg", s_)])
                        op("dve", lambda e, b=b, s_=s_, a_=a_: e.tensor_tensor(out=actb[:, a_, :], in0=psum[b][:, T2:2 * T2], in1=sg[:, s_, :], op=ALU.mult),
                           reads=[("ps", b), ("sg", s_)], writes=[("act", a_)])
                    jj = j - LD
                    if jj >= 0:
                        a_ = acts[jj]
                        for oc in range(8):
                            db = oc // 2
                            half = oc % 2
                            op("pe", lambda e, jj=jj, oc=oc, db=db, half=half, a_=a_: e.matmul(
                                psum[db][:, half * T2:(half + 1) * T2], lhsT=Wd[:, jj, oc * 128:(oc + 1) * 128], rhs=actb[:, a_, :],
                                start=(jj == 0 and half == 0), stop=(jj == 21), skip_group_check=True),
                               reads=[("Wd", jgrp(jj)), ("act", a_)], writes=[("ps", db)])
                    for f in sched.get(j, ()):
                        f()
                for db in range(4):
                    op("dve", lambda e, db=db: e.tensor_tensor(out=hh[:, sl, 2 * db:2 * db + 2, :],
                                                              in0=psum[db][:, :].rearrange("p (a t) -> p a t", a=2),
                                                              in1=hh[:, sl, 2 * db:2 * db + 2, :], op=ALU.add),
                       reads=[("ps", db), ("hh", sl, 2 * db), ("hh", sl, 2 * db + 1)], writes=[("hh", sl, 2 * db), ("hh", sl, 2 * db + 1)])

            stage_a(0)
            stage_a1(0)
            stage_a2(0)
            prev_b = None
            for i in range(NT2):
                sched = {}
                if prev_b is not None:
                    for k, f in prev_b:
                        sched.setdefault(k, []).append(f)
                if i + 1 < NT2:
                    sched.setdefault(14, []).append(lambda i=i: stage_a(i + 1))
                    sched.setdefault(19, []).append(lambda i=i: stage_a1(i + 1))
                    sched.setdefault(21, []).append(lambda i=i: stage_a2(i + 1))
                sched.setdefault(20, []).append(lambda i=i: pin_load(i))
                ffn_loop(i, sched)
                prev_b = stage_b(i)
            for k, f in prev_b:
                f()

        P.emit()
    return nc


_NC_CACHE = {}


def _gpack(inp):
    gpk = np.zeros((128, NG), np.float32)

    def put(name, vec):
        v = np.asarray(vec, np.float32).reshape(-1)
        n = v.size // 128
        gpk[:, GC[name]:GC[name] + n] = v.reshape(n, 128).T

    put("g_mix", inp["g_mix"][0]); put("g_ffn", inp["g_ffn"][0]); put("g_ple_in", inp["g_ple_in"][0])
    put("g_ple_post", inp["g_ple_post"][0]); put("g_final", inp["g_final"])
    cw = np.asarray(inp["conv_w"][0], np.float32)
    for k in range(4):
        gpk[:, GC["conv_w"] + 4 * k:GC["conv_w"] + 4 * k + 4] = cw[k].reshape(4, 128).T
    put("conv_b", inp["conv_b"][0]); put("b_a", inp["b_rg_a"][0]); put("b_x", inp["b_rg_x"][0]); put("lru", inp["lru_L"][0])
    put("g_q", inp["g_q_lat"][0]); put("g_kv", inp["g_kv_lat"][0]); put("g_rnn", inp["g_out_rnn"][0]); put("g_att", inp["g_out_att"][0])
    inv = (10000.0 ** (-np.arange(0, 32, 2, dtype=np.float32) / 32)).astype(np.float32)
    gpk[:, GC["inv"]] = np.tile(np.concatenate([inv, inv]), 4)
    gpk[:, GC["sgn"]] = np.tile(np.concatenate([-np.ones(16, np.float32), np.ones(16, np.float32)]), 4)
    return gpk


def kernel(**inp):
    x = np.asarray(inp["x"], np.float32)
    p = np.asarray(inp["p"], np.float32)[0]
    positions = np.asarray(inp["positions"], np.int32)
    if "nc" not in _NC_CACHE:
        _NC_CACHE["nc"] = build_program()
    nc = _NC_CACHE["nc"]
    gpk = _gpack(inp)
    shared = {
        "gpk": gpk,
        "w_in": np.ascontiguousarray(inp["w_in"][0], np.float32),
        "w_rg_a": np.ascontiguousarray(inp["w_rg_a"][0], np.float32),
        "w_rg_x": np.ascontiguousarray(inp["w_rg_x"][0], np.float32),
        "w_q_up": np.ascontiguousarray(inp["w_q_up"][0], np.float32),
        "w_kv_up": np.ascontiguousarray(inp["w_kv_up"][0], np.float32),
        "w_out": np.ascontiguousarray(inp["w_out"][0], np.float32),
        "w_g": np.ascontiguousarray(inp["w_ffn_gate"][0], np.float32),
        "w_u": np.ascontiguousarray(inp["w_ffn_up"][0], np.float32),
        "w_d": np.ascontiguousarray(inp["w_ffn_down"][0], np.float32),
        "w_pg": np.ascontiguousarray(inp["w_ple_gate"][0], np.float32),
        "w_pp": np.ascontiguousarray(inp["w_ple_proj"][0], np.float32),
    }
    in_maps = []
    for c in range(NCORES):
        m = dict(shared)
        m["xT"] = np.ascontiguousarray(x[2 * c:2 * c + 2].reshape(NTOK, D).T)
        m["pT"] = np.ascontiguousarray(p[2 * c:2 * c + 2].reshape(NTOK, 256).T)
        m["pos"] = np.ascontiguousarray(positions[2 * c:2 * c + 2])
        in_maps.append(m)
    res = run_bass_kernel_spmd(nc, in_maps, core_ids=list(range(NCORES)))
    out = np.empty((16, SEQ, D), np.float32)
    for c in range(NCORES):
        out[2 * c:2 * c + 2] = np.asarray(res.results[c]["outT"]).T.reshape(2, SEQ, D)
    return out
```
